# Optimizing a Trainium2 kernel written in Bass

```python
import math
import jax, jax.numpy as jnp
from jax import lax
import numpy as np

D_MODEL = 1024
BATCH = 2
SEQ = 8192
DEPTH = 1

CHUNK = 64
QBLOCK = 128
MIX_WIDTH = D_MODEL
DIFF_HEADS = 4
DIFF_DH = 64
DIFF_VD = 2 * DIFF_DH
FOX_HEADS = 8
FOX_DH = 64
ROPE_THETA = 500000.0
ROPE_DIM = DIFF_DH // 4
D_FF = ((8 * D_MODEL // 3 + 255) // 256) * 256
EPS = 1e-5
NEG = -1e30

DIFF_QK_W = DIFF_HEADS * 2 * DIFF_DH
DIFF_V_W = DIFF_HEADS * DIFF_VD
FOX_W = FOX_HEADS * FOX_DH
PROJ_W = 2 * DIFF_QK_W + DIFF_V_W + 3 * FOX_W + FOX_HEADS
SPLITS = (DIFF_QK_W,
          2 * DIFF_QK_W,
          2 * DIFF_QK_W + DIFF_V_W,
          2 * DIFF_QK_W + DIFF_V_W + FOX_W,
          2 * DIFF_QK_W + DIFF_V_W + 2 * FOX_W,
          2 * DIFF_QK_W + DIFF_V_W + 3 * FOX_W)

kernel_name = "hybrid_diffattn_forgetting_attn_block"


def rmsnorm(x, g):
    xf = x.astype(jnp.float32)
    y = xf * lax.rsqrt(jnp.mean(xf * xf, axis=-1, keepdims=True) + EPS)
    return (y * g.astype(jnp.float32)).astype(x.dtype)


def rope_partial(t, cos, sin):
    half = ROPE_DIM // 2
    shp = (1, t.shape[1]) + (1,) * (t.ndim - 3) + (half,)
    c = cos.reshape(shp).astype(t.dtype)
    s = sin.reshape(shp).astype(t.dtype)
    t1 = t[..., :half]
    t2 = t[..., half:ROPE_DIM]
    return jnp.concatenate([t1 * c - t2 * s, t2 * c + t1 * s, t[..., ROPE_DIM:]], axis=-1)


def to_blocks(t, nb):
    b, _, h, d = t.shape
    return t.reshape(b, nb, QBLOCK, h, d).transpose(1, 0, 3, 2, 4)


def from_blocks(t):
    nb, b, h, q, d = t.shape
    return t.transpose(1, 0, 3, 2, 4).reshape(b, nb * q, h * d)


def setup_inputs(seed: int = 0) -> dict:
    key = jax.random.key(seed)
    ks = jax.random.split(key, 16)
    f32 = jnp.float32
    nrm = lambda k, shp, s: jax.random.normal(k, shp, f32) * s
    return {
        "x": jax.random.normal(ks[0], (BATCH, SEQ, D_MODEL), f32),
        "norm1_g": 1.0 + nrm(ks[1], (DEPTH, D_MODEL), 0.02),
        "w_in": nrm(ks[2], (DEPTH, D_MODEL, PROJ_W), D_MODEL ** -0.5),
        "b_f": 3.0 + nrm(ks[3], (DEPTH, FOX_HEADS), 0.5),
        "lam_q1": nrm(ks[4], (DEPTH, DIFF_DH), 0.1),
        "lam_k1": nrm(ks[5], (DEPTH, DIFF_DH), 0.1),
        "lam_q2": nrm(ks[6], (DEPTH, DIFF_DH), 0.1),
        "lam_k2": nrm(ks[7], (DEPTH, DIFF_DH), 0.1),
        "subln_g": 1.0 + nrm(ks[8], (DEPTH, DIFF_VD), 0.02),
        "w_out": nrm(ks[9], (DEPTH, MIX_WIDTH, D_MODEL), MIX_WIDTH ** -0.5),
        "norm2_g": 1.0 + nrm(ks[10], (DEPTH, D_MODEL), 0.02),
        "w_gate": nrm(ks[11], (DEPTH, D_MODEL, D_FF), D_MODEL ** -0.5),
        "w_up": nrm(ks[12], (DEPTH, D_MODEL, D_FF), D_MODEL ** -0.5),
        "w_down": nrm(ks[13], (DEPTH, D_FF, D_MODEL), D_FF ** -0.5),
        "normf_g": 1.0 + nrm(ks[14], (D_MODEL,), 0.02),
    }


def reference(x, norm1_g, w_in, b_f, lam_q1, lam_k1, lam_q2, lam_k2, subln_g, w_out,
              norm2_g, w_gate, w_up, w_down, normf_g):
    f32 = jnp.float32
    B, S, _ = x.shape
    nb = S // QBLOCK
    pos = jnp.arange(S, dtype=jnp.int32)
    inv_freq = ROPE_THETA ** (-jnp.arange(0, ROPE_DIM, 2, dtype=f32) / ROPE_DIM)
    ang = pos.astype(f32)[:, None] * inv_freq[None, :]
    cos, sin = jnp.cos(ang), jnp.sin(ang)
    k_chunk = pos // CHUNK
    scale_d = DIFF_DH ** -0.5
    scale_f = FOX_DH ** -0.5

    for l in range(DEPTH):
        lam_init = 0.8 - 0.6 * math.exp(-0.3 * l)
        h = rmsnorm(x, norm1_g[l])
        proj = jnp.einsum('bsd,de->bse', h, w_in[l])
        dq, dk, dv, fq, fk, fv, fg = jnp.split(proj, SPLITS, axis=-1)

        dq = rope_partial(dq.reshape(B, S, DIFF_HEADS, 2, DIFF_DH), cos, sin)
        dk = rope_partial(dk.reshape(B, S, DIFF_HEADS, 2, DIFF_DH), cos, sin)
        dv_t = dv.reshape(B, S, DIFF_HEADS, DIFF_VD).transpose(0, 2, 1, 3)
        k1 = dk[:, :, :, 0].transpose(0, 2, 1, 3)
        k2 = dk[:, :, :, 1].transpose(0, 2, 1, 3)
        lam = (jnp.exp(jnp.sum(lam_q1[l].astype(f32) * lam_k1[l].astype(f32)))
               - jnp.exp(jnp.sum(lam_q2[l].astype(f32) * lam_k2[l].astype(f32)))
               + lam_init)

        fq = fq.reshape(B, S, FOX_HEADS, FOX_DH)
        fk_t = fk.reshape(B, S, FOX_HEADS, FOX_DH).transpose(0, 2, 1, 3)
        fv_t = fv.reshape(B, S, FOX_HEADS, FOX_DH).transpose(0, 2, 1, 3)
        log_f = jax.nn.log_sigmoid((fg + b_f[l]).astype(f32))
        cum = jnp.cumsum(log_f, axis=1).transpose(0, 2, 1)
        cum_blocks = cum.reshape(B, FOX_HEADS, nb, QBLOCK).transpose(2, 0, 1, 3)

        def block(args):
            i, q1b, q2b, qfb, cqb = args
            qpos = i * QBLOCK + jnp.arange(QBLOCK, dtype=jnp.int32)
            chunk_mask = k_chunk[None, :] <= (qpos // CHUNK)[:, None]
            causal = pos[None, :] <= qpos[:, None]
            s1 = jnp.einsum('bhqd,bhkd->bhqk', q1b, k1, preferred_element_type=f32) * scale_d
            s2 = jnp.einsum('bhqd,bhkd->bhqk', q2b, k2, preferred_element_type=f32) * scale_d
            p = (jax.nn.softmax(jnp.where(chunk_mask, s1, NEG), axis=-1)
                 - lam * jax.nn.softmax(jnp.where(chunk_mask, s2, NEG), axis=-1))
            od = jnp.einsum('bhqk,bhkd->bhqd', p.astype(dv_t.dtype), dv_t)
            sf = (jnp.einsum('bhqd,bhkd->bhqk', qfb, fk_t, preferred_element_type=f32) * scale_f
                  + cqb[:, :, :, None] - cum[:, :, None, :])
            pf = jax.nn.softmax(jnp.where(causal, sf, NEG), axis=-1)
            of = jnp.einsum('bhqk,bhkd->bhqd', pf.astype(fv_t.dtype), fv_t)
            return od, of

        od_b, of_b = lax.map(block, (jnp.arange(nb, dtype=jnp.int32),
                                     to_blocks(dq[:, :, :, 0], nb),
                                     to_blocks(dq[:, :, :, 1], nb),
                                     to_blocks(fq, nb),
                                     cum_blocks))
        od_b = rmsnorm(od_b, subln_g[l]) * (1.0 - lam_init)
        mix = jnp.concatenate([from_blocks(od_b), from_blocks(of_b)], axis=-1)
        x = x + jnp.einsum('bse,ed->bsd', mix, w_out[l])

        h2 = rmsnorm(x, norm2_g[l])
        g = jnp.einsum('bsd,df->bsf', h2, w_gate[l])
        u = jnp.einsum('bsd,df->bsf', h2, w_up[l])
        x = x + jnp.einsum('bsf,fd->bsd', jax.nn.silu(g) * u, w_down[l])

    return rmsnorm(x, normf_g)
```

```python
from contextlib import ExitStack

import ml_dtypes
import numpy as np

import concourse.bass as bass
import concourse.mybir as mybir
from concourse.bass_utils import run_bass_kernel_spmd

F32 = mybir.dt.float32
BF16 = mybir.dt.bfloat16
AF = mybir.ActivationFunctionType
ALU = mybir.AluOpType
AX = mybir.AxisListType

D = 1024
DFF = 2816
NM = DFF // 128
EPS = 1e-5
LAM_INIT = 0.8 - 0.6 * 1.0
NEGM = -30000.0
ENGS = ("pe", "act", "dve", "pool", "sp")
C_DQ, C_DQS, C_DK, C_DKS, C_FQ0, C_FQ1, C_FK0, C_FK1, C_V, NCOL = 0, 128, 256, 384, 512, 576, 640, 710, 780, 1036


class Sem:
    def __init__(self, h, eng):
        self.h = h
        self.eng = eng
        self.n = 0


class Prog:
    def __init__(self, nc, stack):
        self.nc = nc
        self.stack = stack
        self.esem = {e: Sem(stack.enter_context(nc.semaphore("s_" + e)), e) for e in ("pe", "act", "dve", "pool")}
        self.all_sems = list(self.esem.values())
        self.waited = {e: {} for e in ENGS}
        self.q = {e: [] for e in ENGS}
        self.lastw = {}
        self.readers = {}

    def new_sem(self, name):
        s = Sem(self.stack.enter_context(self.nc.semaphore(name)), None)
        self.all_sems.append(s)
        return s

    def _emit_waits(self, eng, evs):
        w = self.waited[eng]
        for s, v in evs:
            if w.get(id(s), 0) < v:
                w[id(s)] = v
                self.q[eng].append(lambda e, h=s.h, v=v: e.wait_ge(h, v))

    def _deps(self, eng, reads, writes):
        evs = []
        for k in reads:
            ev = self.lastw.get(k)
            if ev is not None and not (eng == "pe" and ev[0].eng == "pe"):
                evs.append(ev)
        for k in writes:
            ev = self.lastw.get(k)
            if ev is not None and not (eng == "pe" and ev[0].eng == "pe"):
                evs.append(ev)
            for ev in self.readers.get(k, ()):
                if not (eng == "pe" and ev[0].eng == "pe"):
                    evs.append(ev)
        return evs

    def _record(self, ev, reads, writes):
        for k in writes:
            self.lastw[k] = ev
            self.readers[k] = []
        for k in reads:
            self.readers.setdefault(k, []).append(ev)

    def op(self, eng, fn, reads=(), writes=(), signal=True):
        self._emit_waits(eng, self._deps(eng, reads, writes))
        if signal:
            s = self.esem[eng]
            s.n += 1
            self.q[eng].append(lambda e, fn=fn, h=s.h: fn(e).then_inc(h, 1))
            self._record((s, s.n), reads, writes)
        else:
            self.q[eng].append(lambda e, fn=fn: fn(e))

    def group(self, eng, fns, reads=(), writes=()):
        self._emit_waits(eng, self._deps(eng, reads, writes))
        for fn in fns[:-1]:
            self.q[eng].append(lambda e, fn=fn: fn(e))
        s = self.esem[eng]
        s.n += 1
        self.q[eng].append(lambda e, fn=fns[-1], h=s.h: fn(e).then_inc(h, 1))
        self._record((s, s.n), reads, writes)

    def dma(self, eng, out, in_, sem, reads=(), writes=(), **kw):
        self._emit_waits(eng, self._deps(eng, reads, writes))
        sem.n += 16
        self.q[eng].append(lambda e, o=out, i=in_, h=sem.h, kw=kw: e.dma_start(out=o, in_=i, **kw).then_inc(h, 16))
        self._record((sem, sem.n), reads, writes)

    def cc(self, fn, sem, reads=(), writes=()):
        self._emit_waits("pool", self._deps("pool", reads, ("cc_chain",)))
        sem.n += 1
        self.q["pool"].append(lambda e, fn=fn, h=sem.h: fn(e).then_inc(h, 1))
        self._record((sem, sem.n), reads, ("cc_chain",) + tuple(writes))

    def raw(self, eng, fn):
        self.q[eng].append(fn)

    def barrier(self, exclude=(), keep_prefix=None):
        ex = {id(s) for s in exclude}
        evs = [(s, s.n) for s in self.all_sems if s.n > 0 and id(s) not in ex]
        for e in ENGS:
            self._emit_waits(e, evs)
        keep = {k: v for k, v in self.lastw.items() if keep_prefix and k.startswith(keep_prefix)}
        self.lastw.clear()
        self.readers.clear()
        self.lastw.update(keep)

    def emit(self):
        q = self.q
        with self.nc.Block() as blk:
            @blk.tensor
            def _(e):
                for f in q["pe"]:
                    f(e)

            @blk.scalar
            def _(e):
                for f in q["act"]:
                    f(e)

            @blk.vector
            def _(e):
                for f in q["dve"]:
                    f(e)

            @blk.gpsimd
            def _(e):
                for f in q["pool"]:
                    f(e)

            @blk.sync
            def _(e):
                for f in q["sp"]:
                    f(e)
        self.q = {e: [] for e in ENGS}


def build(S, debug=False):
    NT = S // 512
    TOKC = S // 4
    NSUB4 = TOKC // 128
    nc = bass.Bass("TRN2", target_bir_lowering=False)

    def din(name, shape, dt=F32):
        return nc.dram_tensor(name, shape, dt, kind="ExternalInput").ap()

    xb = din("xb", [S, D])
    xres = din("xres", [TOKC, D])
    w_in = din("w_in", [D, NCOL])
    g1T = din("g1T", [128, 8])
    g2T = din("g2T", [128, 8])
    bfp = din("bfp", [128, 2])
    lamv = din("lamv", [128, 256])
    sgv = din("sgv", [128, 1])
    pcv = din("pcv", [128, 6])
    ohv = din("ohv", [128, 4])
    gfb = din("gfb", [128, D])
    ropeC = din("ropeC", [128, S])
    ropeS = din("ropeS", [128, S])
    cbf = din("cbf", [128, 384], BF16)
    swfd = din("swfd", [128, 128])
    w_out = din("w_out", [D, D])
    w_gate = din("w_gate", [D, DFF])
    w_up = din("w_up", [D, DFF])
    w_down = din("w_down", [DFF, D])
    out = nc.dram_tensor("out", [TOKC, D], F32, kind="ExternalOutput").ap()
    CH = min(1024, S)
    NG = S // CH
    mix_in = [nc.dram_tensor(f"mix_in{g}", [256, CH], BF16) for g in range(NG)]
    mix_all = [nc.dram_tensor(f"mix_all{g}", [1024, CH], BF16) for g in range(NG)]
    CW = [768, 640, 768, 640]
    CS = [0, 768, 1408, 2176]
    WoS = nc.dram_tensor("WoS", [128, 8, D], BF16)
    WgS = [nc.dram_tensor(f"WgS{c}", [128, 8, CW[c]], BF16) for c in range(4)]
    WuS = [nc.dram_tensor(f"WuS{c}", [128, 8, CW[c]], BF16) for c in range(4)]
    WdS = nc.dram_tensor("WdS", [128, NM, D], BF16)
    dbg_mix = nc.dram_tensor("dbg_mix", [256, S], BF16, kind="ExternalOutput").ap() if debug else None

    with ExitStack() as gst:
        P = Prog(nc, gst)
        sb = lambda st, name, shape, dt: st.enter_context(nc.sbuf_tensor(name, shape, dt))
        ps = lambda st, name, shape, dt=F32: st.enter_context(nc.psum_tensor(name, shape, dt))

        cb = sb(gst, "cb", [128, 384], BF16)
        ident, maskc, maskd = cb[:, 0:128], cb[:, 128:256], cb[:, 256:384]
        onesb = sb(gst, "onesb", [128, 128], BF16)
        onesf = sb(gst, "onesf", [128, 128], F32)
        swf = sb(gst, "swf", [128, 128], F32)
        g1s = sb(gst, "g1s", [128, 8], F32)
        g2s = sb(gst, "g2s", [128, 8], F32)
        negb = sb(gst, "negb", [128, 2], F32)
        lams = sb(gst, "lams", [128, 256], F32)
        lamp = sb(gst, "lamp", [128, 128], F32)
        lsc = sb(gst, "lsc", [128, 8], F32)
        pc = sb(gst, "pc", [128, 6], F32)
        oh = sb(gst, "oh", [128, 4], F32)
        neglam = lsc[:, 4:5]
        sg = lsc[:, 5:6]
        csem = P.new_sem("csem")
        ccsems = [P.new_sem(f"ccsem{g}") for g in range(NG)]

        WA = sb(gst, "WA", [128, 32768], BF16)
        QdT = WA[:, 0:S]
        KdT = WA[:, 8192:8192 + S]
        Wo = WA[:, 0:8192].rearrange("p (k c) -> p k c", k=8)
        WgL, WuL = [None] * 4, [None] * 4
        WgL[0] = WA[:, 8192:14336].rearrange("p (k c) -> p k c", k=8)
        WuL[0] = WA[:, 14336:20480].rearrange("p (k c) -> p k c", k=8)
        WgL[1] = WA[:, 20480:25600].rearrange("p (k c) -> p k c", k=8)
        WuL[1] = WA[:, 25600:30720].rearrange("p (k c) -> p k c", k=8)
        with ExitStack() as ast:
            QfT = [WA[:, 16384:16384 + S], sb(ast, "QfT1", [128, S], BF16)]
            KfT = [WA[:, 24576:24576 + S], sb(ast, "KfT1", [128, S], BF16)]
            Vall = sb(ast, "Vall", [128, S // 128, 256], BF16)

            with ExitStack() as p1st:
                w_bf = sb(p1st, "w_bf", [128, 8, NCOL], BF16)
                with ExitStack() as st:
                    NSTG = 4
                    stg = [sb(st, f"stg{i}", [128, NCOL], F32) for i in range(NSTG)]
                    ssem = [P.new_sem(f"stg_s{i}") for i in range(NSTG)]
                    for (dst, src) in ((cb, cbf), (g1s, g1T), (g2s, g2T), (negb, bfp), (lams, lamv), (pc, pcv), (oh, ohv), (swf, swfd)):
                        P.dma("sp", dst[:, :], src[:, :], csem, writes=("consts",))
                    P.dma("sp", lsc[:, 5:6], sgv[:, :], csem, writes=("consts",))
                    P.op("dve", lambda e: e.memset(onesb[:, :], 1.0), writes=("onesb",))
                    P.op("dve", lambda e: e.memset(onesf[:, :], 1.0), writes=("onesf",))
                    P.op("dve", lambda e: e.tensor_scalar(out=negb[64:70, :], in0=negb[64:70, :], scalar1=-1.0, scalar2=None, op0=ALU.mult),
                         reads=("consts",), writes=("negb",))
                    P.op("dve", lambda e: e.tensor_scalar(out=lsc[:, 5:6], in0=lsc[:, 5:6], scalar1=1.0 - LAM_INIT, scalar2=None, op0=ALU.mult),
                         reads=("consts",), writes=("sg",))
                    P.op("dve", lambda e: e.tensor_tensor(out=lamp[:, 0:64], in0=lams[:, 0:64], in1=lams[:, 64:128], op=ALU.mult),
                         reads=("consts",), writes=("lamp0",))
                    P.op("dve", lambda e: e.tensor_tensor(out=lamp[:, 64:128], in0=lams[:, 128:192], in1=lams[:, 192:256], op=ALU.mult),
                         reads=("consts",), writes=("lamp1",))
                    P.op("dve", lambda e: e.reduce_sum(out=lsc[:, 0:1], in_=lamp[:, 0:64], axis=AX.X), reads=("lamp0",), writes=("ls0",))
                    P.op("dve", lambda e: e.reduce_sum(out=lsc[:, 1:2], in_=lamp[:, 64:128], axis=AX.X), reads=("lamp1",), writes=("ls1",))
                    P.op("act", lambda e: e.activation(out=lsc[:, 2:4], in_=lsc[:, 0:2], func=AF.Exp), reads=("ls0", "ls1"), writes=("le",))
                    P.op("dve", lambda e: e.tensor_tensor(out=lsc[:, 4:5], in0=lsc[:, 3:4], in1=lsc[:, 2:3], op=ALU.subtract),
                         reads=("le",), writes=("nl0",))
                    P.op("dve", lambda e: e.tensor_scalar(out=lsc[:, 4:5], in0=lsc[:, 4:5], scalar1=-LAM_INIT, scalar2=None, op0=ALU.add),
                         reads=("nl0",), writes=("neglam",))
                    for kc in range(8):
                        sl = kc % NSTG
                        P.dma("act" if kc % 2 == 0 else "sp", stg[sl][:, :], w_in[kc * 128:(kc + 1) * 128, :], ssem[sl], writes=(f"stg{sl}",))
                        eng = "dve"
                        P.op(eng, lambda e, kc=kc, sl=sl: e.tensor_scalar(out=w_bf[:, kc, :], in0=stg[sl][:, :], scalar1=g1s[:, kc:kc + 1],
                                                                           scalar2=None, op0=ALU.mult),
                             reads=(f"stg{sl}", "consts"), writes=("w_bf",))
                    P.barrier()
                    P.emit()

                with ExitStack() as st:
                    xsb = [sb(st, f"xsb{i}", [128, D], F32) for i in range(2)]
                    xsem = [P.new_sem(f"xsem{i}") for i in range(2)]
                    xn = [sb(st, f"xn{i}", [128, D], BF16) for i in range(2)]
                    hT = [sb(st, f"hT{i}", [128, 8, 512], BF16) for i in range(2)]
                    cs = [sb(st, f"cs{i}", [128, 2, 512], F32) for i in range(2)]
                    cssem = [P.new_sem(f"cssem{i}") for i in range(2)]
                    t1 = sb(st, "t1", [128, 512], F32)
                    t2 = sb(st, "t2", [128, 512], F32)
                    zeros = sb(st, "zeros", [128, 512], F32)
                    spt = sb(st, "spt", [128, 512], F32)
                    cum = [[sb(st, f"cum{h}_{i}", [128, 512], F32) for i in range(2)] for h in range(2)]
                    hib = sb(st, "hib", [128, 512], BF16)
                    r1n = sb(st, "r1n", [128, 512], F32)
                    mnb = sb(st, "mnb", [128, 512], BF16)
                    r2 = r1n
                    stat = sb(st, "stat", [128, 8], F32)
                    ptr = [ps(st, f"ptr{i}", [128, 8, 128], BF16) for i in range(2)]
                    pj = [ps(st, f"pj{i}", [128, 512]) for i in range(6)]
                    pjn = [0]

                    def nextpj():
                        i = pjn[0] % 6
                        pjn[0] += 1
                        return pj[i], f"pj{i}"

                    P.op("pool", lambda e: e.memset(zeros[:, :], 0.0), writes=("zeros",))

                    def stage_a(t):
                        sl = t % 2
                        sc = 3 * (t % 2)
                        P.dma("sp", xsb[sl][:, :], xb[t * 128:(t + 1) * 128, :], xsem[sl], writes=(f"xsb{sl}",))
                        P.op("act", lambda e: e.activation(out=xn[sl][:, :], in_=xsb[sl][:, :], func=AF.Square, accum_out=stat[:, sc:sc + 1]),
                             reads=(f"xsb{sl}",), writes=(f"xn{sl}", f"ss{sl}"))
                        P.op("act", lambda e: e.activation(out=stat[:, sc + 1:sc + 2], in_=stat[:, sc:sc + 1], func=AF.Ln, scale=1.0 / D, bias=EPS),
                             reads=(f"ss{sl}",), writes=(f"lnv{sl}",))
                        P.op("act", lambda e: e.activation(out=stat[:, sc + 2:sc + 3], in_=stat[:, sc + 1:sc + 2], func=AF.Exp, scale=-0.5),
                             reads=(f"lnv{sl}",), writes=(f"rstd{sl}",))
                        P.op("dve", lambda e: e.tensor_scalar(out=xn[sl][:, :], in0=xsb[sl][:, :], scalar1=stat[:, sc + 2:sc + 3],
                                                              scalar2=None, op0=ALU.mult),
                             reads=(f"xsb{sl}", f"rstd{sl}"), writes=(f"xn{sl}",))

                    def stage_b(t):
                        sl = t % 2
                        sub = t % 4
                        hs = (t // 4) % 2
                        pt_ = ptr[t % 2]
                        fns = [(lambda e, kc=kc: e.transpose(out=pt_[:, kc, :], in_=xn[sl][:, kc * 128:(kc + 1) * 128], identity=ident)) for kc in range(8)]
                        P.group("pe", fns, reads=(f"xn{sl}",), writes=(f"ptr{t % 2}",))
                        if sub % 2 == 0:
                            P.op("act", lambda e: e.activation(out=hT[hs][:, :, sub * 128:(sub + 1) * 128], in_=pt_[:, :, :], func=AF.Copy),
                                 reads=(f"ptr{t % 2}",), writes=(f"hT{hs}_{sub}",))
                        else:
                            P.op("dve", lambda e: e.tensor_copy(out=hT[hs][:, :, sub * 128:(sub + 1) * 128], in_=pt_[:, :, :]),
                                 reads=(f"ptr{t % 2}",), writes=(f"hT{hs}_{sub}",))

                    def stage_c(i):
                        c0 = i * 512
                        csl = i % 2
                        hs = i % 2
                        hcur = hT[hs]
                        hkeys = tuple(f"hT{hs}_{s_}" for s_ in range(4))

                        def proj_fm(col0, M):
                            pt, key = nextpj()
                            fns = [(lambda e, kc=kc: e.matmul(pt[0:M, :], lhsT=w_bf[:, kc, col0:col0 + M], rhs=hcur[:, kc, :],
                                                              start=(kc == 0), stop=(kc == 7))) for kc in range(8)]
                            P.group("pe", fns, reads=hkeys, writes=(key,))
                            return pt, key

                        def vpart(sub):
                            t = 4 * i + sub
                            pt, key = nextpj()
                            fns = [(lambda e, kc=kc: e.matmul(pt[:, 0:256], lhsT=hcur[:, kc, sub * 128:(sub + 1) * 128],
                                                              rhs=w_bf[:, kc, C_V:C_V + 256], start=(kc == 0), stop=(kc == 7))) for kc in range(8)]
                            P.group("pe", fns, reads=(f"hT{hs}_{sub}",), writes=(key,))
                            P.op("act", lambda e: e.activation(out=Vall[:, t, :], in_=pt[:, 0:256], func=AF.Copy), reads=(key,))

                        def rope(cq, cqs, dst):
                            pa, ka = proj_fm(cq, 128)
                            pb_, kb = proj_fm(cqs, 128)
                            P.op("dve", lambda e: e.tensor_tensor(out=t1[:, :], in0=pa[:, :], in1=cs[csl][:, 0, :], op=ALU.mult),
                                 reads=(ka, f"cs{csl}"), writes=("t1",))
                            P.op("dve", lambda e: e.tensor_tensor(out=t2[:, :], in0=pb_[:, :], in1=cs[csl][:, 1, :], op=ALU.mult),
                                 reads=(kb, f"cs{csl}"), writes=("t2",))
                            P.op("pool", lambda e: e.tensor_tensor(out=dst[:, c0:c0 + 512], in0=t1[:, :], in1=t2[:, :], op=ALU.add),
                                 reads=("t1", "t2"))

                        def foxq(h):
                            pa, ka = proj_fm(C_FQ0 + 64 * h, 128)
                            P.op("act", lambda e: e.activation(out=QfT[h][0:64, c0:c0 + 512], in_=pa[0:64, :], func=AF.Copy, scale=0.125), reads=(ka,))

                        def foxk(h):
                            pa, ka = proj_fm(C_FK0 + 70 * h, 128)
                            P.op("act", lambda e: e.activation(out=KfT[h][0:64, c0:c0 + 512], in_=pa[0:64, :], func=AF.Copy), reads=(ka,))
                            R = slice(64, 70)
                            P.op("act", lambda e: e.activation(out=spt[R, :], in_=pa[R, :], func=AF.Exp, scale=-1.0, bias=negb[R, h:h + 1]),
                                 reads=(ka, "negb"), writes=("spt",))
                            P.op("act", lambda e: e.activation(out=spt[R, :], in_=spt[R, :], func=AF.Ln, bias=1.0), reads=("spt",), writes=("spt",))
                            cur, prev = cum[h][i % 2], cum[h][(i + 1) % 2]
                            init = 0.0 if i == 0 else prev[R, 511:512]
                            ck = f"cum{h}_{i % 2}"
                            P.op("dve", lambda e: e.tensor_tensor_scan(out=cur[R, :], data0=spt[R, :], data1=zeros[R, :], initial=init,
                                                                       op0=ALU.add, op1=ALU.add),
                                 reads=("spt", "zeros", f"cum{h}_{(i + 1) % 2}"), writes=(ck,))
                            P.op("pool", lambda e: e.tensor_copy(out=hib[R, :], in_=cur[R, :]), reads=(ck,), writes=("hib",))
                            P.op("dve", lambda e: e.scalar_tensor_tensor(out=r1n[R, :], in0=hib[R, :], scalar=pc[R, 0:1], in1=cur[R, :],
                                                                         op0=ALU.mult, op1=ALU.subtract), reads=("hib", ck), writes=("r1n",))
                            P.op("pool", lambda e: e.tensor_copy(out=mnb[R, :], in_=r1n[R, :]), reads=("r1n",), writes=("mnb",))
                            P.op("dve", lambda e: e.scalar_tensor_tensor(out=r2[R, :], in0=mnb[R, :], scalar=pc[R, 1:2], in1=r1n[R, :],
                                                                         op0=ALU.mult, op1=ALU.subtract), reads=("mnb", "r1n"), writes=("r1n",))
                            P.op("dve", lambda e: e.tensor_scalar(out=KfT[h][R, c0:c0 + 512], in0=r2[R, :], scalar1=pc[R, 2:3], scalar2=pc[R, 3:4],
                                                                  op0=ALU.mult, op1=ALU.add), reads=("r1n",))
                            P.op("dve", lambda e: e.tensor_scalar(out=QfT[h][R, c0:c0 + 512], in0=r2[R, :], scalar1=pc[R, 4:5], scalar2=pc[R, 5:6],
                                                                  op0=ALU.mult, op1=ALU.add), reads=("r1n",))

                        def part0():
                            P.dma("pool", cs[csl][:, 0, :], ropeC[:, c0:c0 + 512], cssem[csl], writes=(f"cs{csl}",))
                            P.dma("pool", cs[csl][:, 1, :], ropeS[:, c0:c0 + 512], cssem[csl], writes=(f"cs{csl}",))
                            foxk(0)
                            vpart(0)
                            rope(C_DQ, C_DQS, QdT)

                        def part1():
                            foxk(1)
                            vpart(1)
                            rope(C_DK, C_DKS, KdT)

                        def part2():
                            vpart(2)
                            foxq(0)

                        def part3():
                            vpart(3)
                            foxq(1)

                        return [part0, part1, part2, part3]

                    stage_a(0)
                    stage_a(1)
                    stage_b(0)
                    stage_a(2)
                    stage_b(1)
                    stage_a(3)
                    stage_b(2)
                    stage_b(3)
                    for i in range(NT):
                        parts = stage_c(i)
                        for sub in range(4):
                            if i + 1 < NT:
                                stage_a(4 * (i + 1) + sub)
                            parts[sub]()
                            if i + 1 < NT and sub >= 1:
                                stage_b(4 * (i + 1) + sub - 1)
                        if i + 1 < NT:
                            stage_b(4 * (i + 1) + 3)
                    P.barrier()
                    P.emit()

            with ExitStack() as st:
                NB = 3
                pt1 = [sb(st, f"pt1_{i}", [128, 512], BF16) for i in range(NB)]
                pt2 = [sb(st, f"pt2_{i}", [128, 512], BF16) for i in range(NB)]
                ft = [sb(st, f"ft{i}", [128, 512], F32) for i in range(6)]
                mo = [sb(st, f"mo{i}", [128, 512], BF16) for i in range(2)]
                mosem = [P.new_sem(f"mosem{i}") for i in range(2)]
                bank = [ps(st, f"bk{i}", [128, 512]) for i in range(8)]
                VF = sb(st, "VF", [128, S // 128, 192], BF16)
                P.op("pool", lambda e: e.tensor_copy(out=VF[:, :, 0:64], in_=Vall[:, :, 128:192]), writes=("VF0",))
                P.op("pool", lambda e: e.tensor_copy(out=VF[:, :, 128:192], in_=Vall[:, :, 192:256]), writes=("VF1",))
                P.op("pool", lambda e: e.memset(VF[:, :, 64:128], 1.0), writes=("VF2",))
                O1, O2, S1, S2 = bank[0], bank[1], bank[2], bank[3]
                s1b, s2b = [bank[4], bank[5]], [bank[6], bank[7]]
                pstg = [sb(st, f"pstg{i}", [128, D], F32) for i in range(2)]
                pcb = [sb(st, f"pcb{i}", [128, D], BF16) for i in range(2)]
                pisem = [P.new_sem(f"pisem{i}") for i in range(2)]
                posem = [P.new_sem(f"posem{i}") for i in range(2)]
                pjobs = []
                for fc in range(8):
                    pjobs.append((w_out[fc * 128:(fc + 1) * 128, :], WoS[:, fc, :], D, (sg if fc % 2 == 0 else None)))
                for c in range(4):
                    for kc in range(8):
                        pjobs.append((w_gate[kc * 128:(kc + 1) * 128, CS[c]:CS[c] + CW[c]], WgS[c][:, kc, :], CW[c], g2s[:, kc:kc + 1]))
                        pjobs.append((w_up[kc * 128:(kc + 1) * 128, CS[c]:CS[c] + CW[c]], WuS[c][:, kc, :], CW[c], g2s[:, kc:kc + 1]))
                NPRE = 8 + 32
                for m in range(NM):
                    pjobs.append((w_down[m * 128:(m + 1) * 128, :], WdS[:, m, :], D, None))
                pstate = {"n": 0, "iters": 0}
                wpsem = [P.new_sem(f"wpsem{i}") for i in range(5)]
                TOT_IT = 3 * sum(4 * Q_ + 4 for Q_ in range(NT))
                PER = max(1, (TOT_IT - 8) // (len(pjobs) + 2))

                def pre_in(n):
                    src, dst, wd, gain = pjobs[n]
                    sl = n % 2
                    P.dma("sp", pstg[sl][:, 0:wd], src, pisem[sl], writes=(f"pstg{sl}",))

                def pre_cast_out(n):
                    src, dst, wd, gain = pjobs[n]
                    sl = n % 2
                    eng = "pool"
                    if gain is None:
                        P.op(eng, lambda e: e.tensor_copy(out=pcb[sl][:, 0:wd], in_=pstg[sl][:, 0:wd]), reads=(f"pstg{sl}",), writes=(f"pcb{sl}",))
                    else:
                        P.op(eng, lambda e: e.tensor_scalar(out=pcb[sl][:, 0:wd], in0=pstg[sl][:, 0:wd], scalar1=gain, scalar2=None, op0=ALU.mult),
                             reads=(f"pstg{sl}",), writes=(f"pcb{sl}",))
                    P.dma("sp", dst, pcb[sl][:, 0:wd], posem[sl], reads=(f"pcb{sl}",))

                def pre_step():
                    n = pstate["n"]
                    if n <= len(pjobs):
                        if n < len(pjobs):
                            pre_in(n)
                        if n >= 1:
                            pre_cast_out(n - 1)
                        pstate["n"] = n + 1

                T_DIFF, T_FOX = 1.84, 0.64
                n_it = sum(4 * Q_ + 4 for Q_ in range(NT))
                PACE = 0.92 * (n_it * T_DIFF + 2 * n_it * T_FOX) / (len(pjobs) + 1)
                pstate["acc"] = 0.0

                def pre_tick(w=T_DIFF):
                    pstate["acc"] += w
                    if pstate["acc"] >= PACE:
                        pstate["acc"] -= PACE
                        pre_step()

                def tiles_of(Q):
                    return [(kt, max(0, kt - 4 * Q)) for kt in range(4 * Q + 4)]

                it = 0
                pend = {"b": None}
                for Q in range(NT):
                    q0 = Q * 512
                    tl = tiles_of(Q)
                    last = len(tl) - 1

                    def scores(idx, it, tl=tl, q0=q0):
                        kt, n0 = tl[idx]
                        sl = it % 2
                        diag = kt >= (q0 // 128)
                        for (sbk, lo, key) in ((s1b[sl], 0, f"s1_{sl}"), (s2b[sl], 64, f"s2_{sl}")):
                            fns = [lambda e, sbk=sbk, lo=lo, kt=kt, n0=n0: e.matmul(sbk[:, n0 * 128:512], lhsT=KdT[lo:lo + 64, kt * 128:(kt + 1) * 128],
                                                                                     rhs=QdT[lo:lo + 64, q0 + n0 * 128:q0 + 512], start=True, stop=not diag)]
                            if diag:
                                fns.append(lambda e, sbk=sbk, n0=n0: e.matmul(sbk[:, n0 * 128:(n0 + 1) * 128], lhsT=ident, rhs=maskd, start=False, stop=True))
                            P.group("pe", fns, writes=(key,))

                    def exps(idx, it, tl=tl):
                        kt, n0 = tl[idx]
                        sl, bl = it % 2, it % NB
                        P.op("act", lambda e: e.activation(out=pt1[bl][:, n0 * 128:512], in_=s1b[sl][:, n0 * 128:512], func=AF.Exp, scale=0.125),
                             reads=(f"s1_{sl}",), writes=(f"pt1_{bl}",))
                        P.op("act", lambda e: e.activation(out=pt2[bl][:, n0 * 128:512], in_=s2b[sl][:, n0 * 128:512], func=AF.Exp, scale=0.125),
                             reads=(f"s2_{sl}",), writes=(f"pt2_{bl}",))

                    def pv(idx, it, tl=tl, last=last):
                        kt, n0 = tl[idx]
                        bl = it % NB
                        st_, sp_ = (idx == 0), (idx == last)
                        cs_ = slice(n0 * 128, 512)
                        fns = [
                            lambda e: e.matmul(O1[:, cs_], lhsT=Vall[:, kt, 0:128], rhs=pt1[bl][:, cs_], start=st_, stop=sp_),
                            lambda e: e.matmul(S1[:, cs_], lhsT=onesb[:, :], rhs=pt1[bl][:, cs_], start=st_, stop=sp_),
                            lambda e: e.matmul(O2[:, cs_], lhsT=Vall[:, kt, 0:128], rhs=pt2[bl][:, cs_], start=st_, stop=sp_),
                            lambda e: e.matmul(S2[:, cs_], lhsT=onesb[:, :], rhs=pt2[bl][:, cs_], start=st_, stop=sp_),
                        ]
                        P.group("pe", fns, reads=(f"pt1_{bl}", f"pt2_{bl}"), writes=("O1", "O2", "S1", "S2"))

                    for idx in range(len(tl)):
                        scores(idx, it + idx)
                        exps(idx, it + idx)
                        if idx >= 1:
                            pv(idx - 1, it + idx - 1)
                        if idx == 2 and pend["b"] is not None:
                            pend["b"]()
                            pend["b"] = None
                        pre_tick()
                    pv(last, it + last)
                    it += len(tl)
                    r1_, o1_, r2_, o2_, od_, sq_ = ft
                    P.op("dve", lambda e: e.tensor_copy(out=r1_[:, :], in_=S1[:, :]), reads=("S1",), writes=("r1",))
                    P.op("dve", lambda e: e.tensor_copy(out=o1_[:, :], in_=O1[:, :]), reads=("O1",), writes=("o1",))
                    P.op("dve", lambda e: e.tensor_copy(out=r2_[:, :], in_=S2[:, :]), reads=("S2",), writes=("r2",))
                    P.op("dve", lambda e: e.tensor_copy(out=o2_[:, :], in_=O2[:, :]), reads=("O2",), writes=("o2",))
                    P.op("dve", lambda e: e.reciprocal(out=r1_[:, :], in_=r1_[:, :]), reads=("r1",), writes=("r1",))
                    P.op("dve", lambda e: e.tensor_tensor(out=o1_[:, :], in0=o1_[:, :], in1=r1_[:, :], op=ALU.mult), reads=("o1", "r1"), writes=("o1",))
                    P.op("dve", lambda e: e.reciprocal(out=r2_[:, :], in_=r2_[:, :]), reads=("r2",), writes=("r2",))
                    P.op("dve", lambda e: e.tensor_tensor(out=o2_[:, :], in0=o2_[:, :], in1=r2_[:, :], op=ALU.mult), reads=("o2", "r2"), writes=("o2",))
                    P.op("dve", lambda e: e.scalar_tensor_tensor(out=od_[:, :], in0=o2_[:, :], scalar=neglam, in1=o1_[:, :], op0=ALU.mult, op1=ALU.add),
                         reads=("o1", "o2"), writes=("od",))
                    P.op("dve", lambda e: e.tensor_tensor(out=sq_[:, :], in0=od_[:, :], in1=od_[:, :], op=ALU.mult), reads=("od",), writes=("sq",))

                    def part_b(Q=Q, q0=q0):
                        ssq = s1b[0]
                        P.group("pe", [lambda e: e.matmul(ssq[:, :], lhsT=onesf[:, :], rhs=sq_[:, :], start=True, stop=True)], reads=("sq",), writes=("s1_0",))
                        P.op("act", lambda e: e.activation(out=r1_[:, :], in_=ssq[:, :], func=AF.Ln, scale=1.0 / 128, bias=EPS), reads=("s1_0",), writes=("r1",))
                        P.op("act", lambda e: e.activation(out=r2_[:, :], in_=r1_[:, :], func=AF.Exp, scale=-0.5), reads=("r1",), writes=("r2",))
                        ms = Q % 2
                        P.op("dve", lambda e: e.tensor_tensor(out=mo[ms][:, :], in0=od_[:, :], in1=r2_[:, :], op=ALU.mult),
                             reads=("od", "r2"), writes=(f"mo{ms}",))
                        P.dma("sp", mix_in[q0 // CH][0:128, q0 % CH:q0 % CH + 512], mo[ms][:, :], mosem[ms], reads=(f"mo{ms}",))

                    pend["b"] = part_b
                if pend["b"] is not None:
                    pend["b"]()
                    pend["b"] = None
                P.barrier()
                Of = [bank[0], bank[1]]
                SWo = [bank[2], bank[3]]
                sfb = [bank[4], bank[5], bank[6], bank[7]]
                for k_ in range(4):
                    P.op("pool", lambda e, k_=k_: e.memset(ft[k_][:, :], 0.0), writes=(f"Rt{k_}",))
                it = 0
                ftail = []
                for h in range(2):
                    vlo = 128 + 64 * h
                    vf0 = 64 * h
                    orows = slice(0, 64) if h == 0 else slice(64, 128)
                    srows = slice(64, 128) if h == 0 else slice(0, 64)
                    for Q in range(NT):
                        q0 = Q * 512
                        tl = tiles_of(Q)
                        last = len(tl) - 1
                        ab = (h * NT + Q) % 2

                        def scores(idx, it, tl=tl, q0=q0, h=h):
                            kt, n0 = tl[idx]
                            sl = it % 4
                            diag = kt >= (q0 // 128)
                            sbk = sfb[sl]
                            fns = [lambda e: e.matmul(sbk[:, n0 * 128:512], lhsT=KfT[h][0:70, kt * 128:(kt + 1) * 128],
                                                      rhs=QfT[h][0:70, q0 + n0 * 128:q0 + 512], start=True, stop=not diag)]
                            if diag:
                                fns.append(lambda e: e.matmul(sbk[:, n0 * 128:(n0 + 1) * 128], lhsT=ident, rhs=maskc, start=False, stop=True))
                            P.group("pe", fns, writes=(f"sf_{sl}",))

                        def exps(idx, it, tl=tl):
                            kt, n0 = tl[idx]
                            sl, bl = it % 4, it % (2 * NB)
                            ptile = (pt1 + pt2)[bl]
                            P.op("act", lambda e: e.activation(out=ptile[:, n0 * 128:512], in_=sfb[sl][:, n0 * 128:512], func=AF.Exp),
                                 reads=(f"sf_{sl}",), writes=(f"pf_{bl}",))

                        def pv(idx, it, tl=tl, last=last, ab=ab, vf0=vf0):
                            kt, n0 = tl[idx]
                            bl = it % (2 * NB)
                            ptile = (pt1 + pt2)[bl]
                            st_, sp_ = (idx == 0), (idx == last)
                            cs_ = slice(n0 * 128, 512)
                            fns = [lambda e: e.matmul(Of[ab][:, cs_], lhsT=VF[:, kt, vf0:vf0 + 128], rhs=ptile[:, cs_], start=st_, stop=sp_)]
                            P.group("pe", fns, reads=(f"pf_{bl}", "VF0", "VF1", "VF2"), writes=(f"Of{ab}",))

                        for idx in range(len(tl)):
                            scores(idx, it + idx)
                            exps(idx, it + idx)
                            if idx >= 2:
                                pv(idx - 2, it + idx - 2)
                            elif ftail:
                                ftail.pop(0)()
                            if idx == 3 and pend["b"] is not None:
                                pend["b"]()
                                pend["b"] = None
                            pre_tick(T_FOX)
                        Rt = ft[2 * h + ab]
                        rb = ft[4 + ab]
                        ms = Q % 2

                        def tail0(pv=pv, it0=it, last=last):
                            pv(last - 1, it0 + last - 1)

                        def tail1(pv=pv, it0=it, last=last, Rt=Rt, ab=ab, srows=srows, h=h):
                            pv(last, it0 + last)
                            P.op("dve", lambda e: e.reciprocal(out=Rt[srows, :], in_=Of[ab][srows, :]),
                                 reads=(f"Of{ab}",), writes=(f"Rt{2 * h + ab}",))

                        it += len(tl)

                        def part_b(Rt=Rt, rb=rb, ab=ab, ms=ms, orows=orows, h=h, Q=Q, q0=q0, vlo=vlo):
                            P.group("pe", [lambda e: e.matmul(SWo[ab][:, :], lhsT=swf[:, :], rhs=Rt[:, :], start=True, stop=True)],
                                    reads=(f"Rt{2 * h + ab}",), writes=(f"SWo{ab}",))
                            P.op("act", lambda e: e.activation(out=rb[orows, :], in_=SWo[ab][orows, :], func=AF.Copy),
                                 reads=(f"SWo{ab}",), writes=(f"rb{ab}",))
                            P.op("dve", lambda e: e.tensor_tensor(out=mo[ms][orows, :], in0=Of[ab][orows, :], in1=rb[orows, :], op=ALU.mult),
                                 reads=(f"Of{ab}", f"rb{ab}"), writes=(f"mo{ms}",))
                            P.dma("sp", mix_in[q0 // CH][vlo:vlo + 64, q0 % CH:q0 % CH + 512], mo[ms][orows, :], mosem[ms], reads=(f"mo{ms}",),
                                  writes=(f"mixin_{h}_{Q}",))
                            if h == 1 and (q0 + 512) % CH == 0:
                                g = q0 // CH
                                deps = tuple(f"mixin_{hh}_{QQ}" for hh in range(2) for QQ in range(g * CH // 512, (g + 1) * CH // 512))
                                P.cc(lambda e: e.collective_compute("AllGather", ALU.bypass, replica_groups=[[0, 1, 2, 3], [4, 5, 6, 7]],
                                                                    ins=[mix_in[g].ap().opt()], outs=[mix_all[g].ap().opt()]), ccsems[g], reads=deps,
                                     writes=(f"ccout{g}",))

                        def tail1b(tail1=tail1, part_b=part_b):
                            tail1()
                            pend["b"] = part_b

                        assert not ftail
                        ftail.extend([tail0, tail1b])
                        if h == 0 and Q == NT - 1:
                            while pstate["n"] <= NPRE:
                                pre_step()
                            evs = [(P.esem["pe"], P.esem["pe"].n)] + [(s_, s_.n) for s_ in posem]
                            P._emit_waits("sp", evs)
                            P.dma("sp", Wo[:, :, :], WoS.ap()[:, :, :], wpsem[0], writes=("Wo",))
                            for c_ in range(2):
                                P.dma("sp", WgL[c_][:, :, :], WgS[c_].ap()[:, :, :], wpsem[1 + 2 * c_], writes=(f"Wg{c_}",))
                                P.dma("sp", WuL[c_][:, :, :], WuS[c_].ap()[:, :, :], wpsem[2 + 2 * c_], writes=(f"Wu{c_}",))
                while ftail:
                    ftail.pop(0)()
                if pend["b"] is not None:
                    pend["b"]()
                    pend["b"] = None
                while pstate["n"] <= len(pjobs):
                    pre_step()
                P.barrier(exclude=ccsems, keep_prefix="cc")
                P.emit()

        if debug:
            dsem = P.new_sem("dsem")
            for g in range(NG):
                P.dma("sp", dbg_mix[:, g * CH:(g + 1) * CH], mix_in[g].ap()[:, :], dsem)
            P.barrier()
            P.emit()

        with ExitStack() as wst:
            for c_ in (2, 3):
                WgL[c_] = sb(wst, f"Wg{c_}", [128, 8, CW[c_]], BF16)
                WuL[c_] = sb(wst, f"Wu{c_}", [128, 8, CW[c_]], BF16)
            Wd = sb(wst, "Wd", [128, NM, D], BF16)
            with ExitStack() as st:
                TW, NS = 256, 2
                NT4 = TOKC // TW
                mc = [sb(st, f"mc{c}", [128, 2, 128], BF16) for c in range(4)]
                mcsem = [P.new_sem(f"mcsem{c}") for c in range(4)]
                sel = [WA[:, 31744:32768].rearrange("p (k c) -> p k c", k=8), sb(st, "sel1", [128, 8, 128], BF16)]
                xq = [sb(st, f"xq{i}", [128, D], F32) for i in range(NS)]
                xqsem = [P.new_sem(f"xqsem{i}") for i in range(NS)]
                xo = sb(st, "xo", [128, D], F32)
                xosem = P.new_sem("xosem")
                x1 = sb(st, "x1", [128, NS, D], F32)
                xn2 = WA[:, 30720:31744]
                h2T = sb(st, "h2T", [128, 8, TW], BF16)
                sgl = sb(st, "sgl", [128, 2, TW], F32)
                aT = sb(st, "aT", [128, NM, TW], BF16)
                gf = sb(st, "gf", [128, D], F32)
                stat = sb(st, "stat4", [128, 8], F32)
                po = [ps(st, f"po{i}", [128, 512]) for i in range(2)]
                ptr = ps(st, "ptr4", [128, 8, 128], BF16)
                pg = [ps(st, f"pg{i}", [128, 512]) for i in range(2)]
                pu = [ps(st, f"pu{i}", [128, 512]) for i in range(2)]
                gsem = P.new_sem("gsem")
                P.dma("sp", gf[:, :], gfb[:, :], gsem, writes=("gf",))
                wsems = [P.new_sem(f"wsem{i}") for i in range(11)]

                def bulk_loads():
                    for c in (2, 3):
                        P.dma("act", WgL[c][:, :, :], WgS[c].ap()[:, :, :], wsems[1 + 2 * c], reads=("sel1_3",), writes=(f"Wg{c}",))
                        P.dma("act", WuL[c][:, :, :], WuS[c].ap()[:, :, :], wsems[2 + 2 * c], writes=(f"Wu{c}",))
                    P.dma("act", Wd[:, 0:11, :], WdS.ap()[:, 0:11, :], wsems[9], writes=("Wd0",))
                    P.dma("act", Wd[:, 11:22, :], WdS.ap()[:, 11:22, :], wsems[10], writes=("Wd1",))
                mall = [m_.ap().rearrange("(fc p) t -> p fc t", p=128) for m_ in mix_all]

                def pre(t):
                    rounds = []
                    for sub in range(NS):
                        T0 = t * TW + sub * 128
                        P.dma("sp", xq[sub][:, :], xres[T0:T0 + 128, :], xqsem[sub], writes=(f"xq{sub}",))
                        for g4 in range(4):
                            def rnd(sub=sub, g4=g4, T0=T0):
                                fs = slice(2 * g4, 2 * g4 + 2)
                                for c in range(4):
                                    tk = c * TOKC + T0
                                    P.dma("sp", mc[c][:, :, :], mall[tk // CH][:, fs, tk % CH:tk % CH + 128], mcsem[c],
                                          reads=(f"ccout{tk // CH}",), writes=(f"mc{c}",))
                                P.op("dve", lambda e: e.tensor_scalar(out=sel[sub][:, fs, :], in0=mc[0][:, :, :], scalar1=oh[:, 0:1],
                                                                      scalar2=None, op0=ALU.mult),
                                     reads=("mc0",), writes=(f"sel{sub}_{g4}",))
                                for c in range(1, 4):
                                    P.op("dve", lambda e, c=c: e.scalar_tensor_tensor(out=sel[sub][:, fs, :], in0=mc[c][:, :, :], scalar=oh[:, c:c + 1],
                                                                                      in1=sel[sub][:, fs, :], op0=ALU.mult, op1=ALU.add),
                                         reads=(f"mc{c}", f"sel{sub}_{g4}"), writes=(f"sel{sub}_{g4}",))
                            rounds.append(rnd)
                    return rounds

                def outproj(t, sub, banks, bkeys):
                    skeys = tuple(f"sel{sub}_{g4}" for g4 in range(4))
                    for hf in range(2):
                        fns = [(lambda e, fc=fc, hf=hf: e.matmul(banks[hf][:, :], lhsT=sel[sub][:, fc, :], rhs=Wo[:, fc, hf * 512:(hf + 1) * 512],
                                                                 start=(fc == 0), stop=(fc == 7))) for fc in range(8)]
                        P.group("pe", fns, reads=skeys + ("Wo",), writes=(bkeys[hf],))
                        P.op("dve", lambda e, hf=hf: e.tensor_tensor(out=x1[:, sub, hf * 512:(hf + 1) * 512], in0=banks[hf][:, :],
                                                                    in1=xq[sub][:, hf * 512:(hf + 1) * 512], op=ALU.add),
                             reads=(bkeys[hf], f"xq{sub}"), writes=(f"x1_{sub}_{hf}",))

                def chain(t, sub):
                    xk = (f"x1_{sub}_0", f"x1_{sub}_1")
                    P.op("act", lambda e: e.activation(out=xn2[:, :], in_=x1[:, sub, :], func=AF.Square, accum_out=stat[:, 0:1]),
                         reads=xk, writes=("xn2", "ss0"))
                    P.op("act", lambda e: e.activation(out=stat[:, 1:2], in_=stat[:, 0:1], func=AF.Ln, scale=1.0 / D, bias=EPS), reads=("ss0",), writes=("ln0",))
                    P.op("act", lambda e: e.activation(out=stat[:, 2:3], in_=stat[:, 1:2], func=AF.Exp, scale=-0.5), reads=("ln0",), writes=("rstd0",))
                    P.op("dve", lambda e: e.tensor_scalar(out=xn2[:, :], in0=x1[:, sub, :], scalar1=stat[:, 2:3], scalar2=None, op0=ALU.mult),
                         reads=xk + ("rstd0",), writes=("xn2",))
                    fns = [(lambda e, kc=kc: e.transpose(out=ptr[:, kc, :], in_=xn2[:, kc * 128:(kc + 1) * 128], identity=ident)) for kc in range(8)]
                    P.group("pe", fns, reads=("xn2",), writes=("ptr4",))
                    P.op("act", lambda e: e.activation(out=h2T[:, :, sub * 128:(sub + 1) * 128], in_=ptr[:, :, :], func=AF.Copy),
                         reads=("ptr4",), writes=(f"h2T{sub}",))

                def mloop(t, hooks=()):
                    hkeys = tuple(f"h2T{s_}" for s_ in range(NS))
                    hooks = list(hooks)
                    for m in range(NM):
                        if m >= 2 and m % 2 == 0 and hooks:
                            hooks.pop(0)()
                        b_ = m % 2
                        wc_ = max(c_ for c_ in range(4) if CS[c_] <= m * 128)
                        lc = m * 128 - CS[wc_]
                        for (W_, pp, key, wn) in ((WgL[wc_], pg[b_], f"pg{b_}", "Wg"), (WuL[wc_], pu[b_], f"pu{b_}", "Wu")):
                            fns = [(lambda e, kc=kc, W_=W_, pp=pp, lc=lc: e.matmul(pp[:, 0:TW], lhsT=W_[:, kc, lc:lc + 128], rhs=h2T[:, kc, :],
                                                                                   start=(kc == 0), stop=(kc == 7))) for kc in range(8)]
                            P.group("pe", fns, reads=hkeys + (f"{wn}{wc_}",), writes=(key,))
                        P.op("act", lambda e, b_=b_: e.activation(out=sgl[:, b_, :], in_=pg[b_][:, 0:TW], func=AF.Silu), reads=(f"pg{b_}",), writes=(f"sgl{b_}",))
                        P.op("dve", lambda e, b_=b_, m=m: e.tensor_tensor(out=aT[:, m, :], in0=pu[b_][:, 0:TW], in1=sgl[:, b_, :], op=ALU.mult),
                             reads=(f"pu{b_}", f"sgl{b_}"), writes=(f"aT{m}",))

                def down(t, sub):
                    akeys = tuple(f"aT{m}" for m in range(NM))
                    T0 = t * TW + sub * 128
                    for hf in range(2):
                        fns = [(lambda e, m=m, hf=hf: e.matmul(po[hf][:, :], lhsT=aT[:, m, sub * 128:(sub + 1) * 128],
                                                               rhs=Wd[:, m, hf * 512:(hf + 1) * 512], start=(m == 0), stop=(m == NM - 1)))
                               for m in range(NM)]
                        P.group("pe", fns, reads=akeys + ("Wd0", "Wd1"), writes=(f"po{hf}",))
                        P.op("dve", lambda e, hf=hf: e.tensor_tensor(out=xo[:, hf * 512:(hf + 1) * 512], in0=po[hf][:, :],
                                                                    in1=x1[:, sub, hf * 512:(hf + 1) * 512], op=ALU.add),
                             reads=(f"po{hf}", f"x1_{sub}_{hf}"), writes=(f"xo_{hf}",))
                    P.op("act", lambda e: e.activation(out=xn2[:, :], in_=xo[:, :], func=AF.Square, accum_out=stat[:, 4:5]),
                         reads=("xo_0", "xo_1"), writes=("xn2", "ss4"))
                    P.op("act", lambda e: e.activation(out=stat[:, 5:6], in_=stat[:, 4:5], func=AF.Ln, scale=1.0 / D, bias=EPS), reads=("ss4",), writes=("ln4",))
                    P.op("act", lambda e: e.activation(out=stat[:, 6:7], in_=stat[:, 5:6], func=AF.Exp, scale=-0.5), reads=("ln4",), writes=("rstd4",))
                    P.op("dve", lambda e: e.scalar_tensor_tensor(out=xo[:, :], in0=xo[:, :], scalar=stat[:, 6:7], in1=gf[:, :], op0=ALU.mult, op1=ALU.mult),
                         reads=("xo_0", "xo_1", "rstd4", "gf"), writes=("xo_0", "xo_1"))
                    P.dma("sp", out[T0:T0 + 128, :], xo[:, :], xosem, reads=("xo_0", "xo_1"))

                r0 = pre(0)
                for r_ in r0:
                    r_()
                bulk_loads()
                outproj(0, 0, po, ("po0", "po1"))
                outproj(0, 1, pg, ("pg0", "pg1"))
                chain(0, 0)
                chain(0, 1)
                for t in range(NT4):
                    nxt = t + 1 < NT4
                    mloop(t, pre(t + 1) if nxt else ())
                    down(t, 0)
                    if nxt:
                        outproj(t + 1, 0, pg, ("pg0", "pg1"))
                    down(t, 1)
                    if nxt:
                        chain(t + 1, 0)
                        outproj(t + 1, 1, pu, ("pu0", "pu1"))
                        chain(t + 1, 1)
                P.barrier()
                P.emit()
    return nc


def _consts(S):
    f32 = np.float32
    inv_freq = (500000.0 ** (-(np.arange(0, 16, 2, dtype=f32) / f32(16)))).astype(f32)
    ang = (np.arange(S, dtype=f32)[:, None] * inv_freq[None, :]).astype(f32)
    cos, sin = np.cos(ang).astype(f32), np.sin(ang).astype(f32)
    C = np.ones((128, S), f32)
    Sg = np.zeros((128, S), f32)
    for p in range(128):
        d = p % 64
        if d < 8:
            C[p] = cos[:, d]
            Sg[p] = -sin[:, d]
        elif d < 16:
            C[p] = cos[:, d - 8]
            Sg[p] = sin[:, d - 8]
    k = np.arange(128)[:, None]
    q = np.arange(128)[None, :]
    ident = (k == q).astype(f32)
    maskc = np.where(k <= q, 0.0, NEGM).astype(f32)
    maskd = np.where((k >= 64) & (q < 64), NEGM, 0.0).astype(f32)
    cbf = np.concatenate([ident, maskc, maskd], axis=1).astype(ml_dtypes.bfloat16)
    pc = np.zeros((128, 6), f32)
    pc[64:70, 0] = (0, 1, 1, 0, 1, 1)
    pc[64:70, 1] = (0, 0, 1, 0, 0, 1)
    pc[64:70, 2] = (1, 1, 1, 0, 0, 0)
    pc[64:70, 3] = (0, 0, 0, 1, 1, 1)
    pc[64:70, 4] = (0, 0, 0, -1, -1, -1)
    pc[64:70, 5] = (1, 1, 1, 0, 0, 0)
    swf = (k == (q + 64) % 128).astype(f32)
    return C, Sg, cbf, pc, swf


def _group_cols(j):
    def swap(base):
        cols = []
        for p in range(128):
            c_, d = divmod(p, 64)
            pd = d + 8 if d < 8 else (d - 8 if d < 16 else d)
            cols.append(base + j * 128 + c_ * 64 + pd)
        return cols

    cols = list(range(j * 128, (j + 1) * 128)) + swap(0)
    cols += list(range(512 + j * 128, 512 + (j + 1) * 128)) + swap(512)
    h0, h1 = 2 * j, 2 * j + 1
    cols += list(range(1536 + h0 * 64, 1536 + h0 * 64 + 64)) + list(range(1536 + h1 * 64, 1536 + h1 * 64 + 64))
    cols += list(range(2048 + h0 * 64, 2048 + h0 * 64 + 64)) + [3072 + h0] * 6
    cols += list(range(2048 + h1 * 64, 2048 + h1 * 64 + 64)) + [3072 + h1] * 6
    cols += list(range(1024 + j * 128, 1024 + (j + 1) * 128))
    cols += list(range(2560 + h0 * 64, 2560 + h0 * 64 + 64)) + list(range(2560 + h1 * 64, 2560 + h1 * 64 + 64))
    assert len(cols) == NCOL
    return np.array(cols)


def make_in_maps(x, norm1_g, w_in, b_f, lam_q1, lam_k1, lam_q2, lam_k2, subln_g, w_out, norm2_g, w_gate, w_up, w_down, normf_g):
    f32 = np.float32
    a = lambda t: np.ascontiguousarray(np.asarray(t, dtype=f32))
    x = a(x)
    B, S, _ = x.shape
    TOKC = S // 4
    C, Sg, cbf, pc, swf = _consts(S)
    w_in0, w_out0 = a(w_in)[0], a(w_out)[0]
    perm = np.concatenate([np.concatenate([np.arange(r * 128, (r + 1) * 128), np.arange(512 + r * 128, 512 + (r + 1) * 128)]) for r in range(4)])
    w_out_p = np.ascontiguousarray(w_out0[perm])
    g1T = np.ascontiguousarray(a(norm1_g)[0].reshape(8, 128).T)
    g2T = np.ascontiguousarray(a(norm2_g)[0].reshape(8, 128).T)
    lamv = np.ascontiguousarray(np.broadcast_to(np.concatenate([a(lam_q1)[0], a(lam_k1)[0], a(lam_q2)[0], a(lam_k2)[0]])[None, :], (128, 256)))
    sgv = np.ascontiguousarray(a(subln_g)[0].reshape(128, 1))
    gfb = np.ascontiguousarray(np.broadcast_to(a(normf_g)[None, :], (128, D)))
    wg, wu, wd = a(w_gate)[0], a(w_up)[0], a(w_down)[0]
    bf0 = a(b_f)[0]
    maps = []
    for c in range(8):
        b, j = divmod(c, 4)
        bfp = np.zeros((128, 2), f32)
        bfp[64:70, 0] = bf0[2 * j]
        bfp[64:70, 1] = bf0[2 * j + 1]
        oh = np.zeros((128, 4), f32)
        oh[:, j] = 1.0
        maps.append({
            "xb": x[b], "xres": np.ascontiguousarray(x[b, j * TOKC:(j + 1) * TOKC]),
            "w_in": np.ascontiguousarray(w_in0[:, _group_cols(j)]),
            "g1T": g1T, "g2T": g2T, "bfp": bfp, "lamv": lamv, "sgv": sgv, "pcv": pc, "ohv": oh, "gfb": gfb,
            "ropeC": C, "ropeS": Sg, "cbf": cbf, "swfd": swf, "w_out": w_out_p, "w_gate": wg, "w_up": wu, "w_down": wd,
        })
    return maps, B, S


_NC_CACHE = {}


def kernel(**inputs):
    maps, B, S = make_in_maps(**inputs)
    if S not in _NC_CACHE:
        _NC_CACHE[S] = build(S)
    res = run_bass_kernel_spmd(_NC_CACHE[S], maps, core_ids=list(range(8)))
    TOKC = S // 4
    outp = np.empty((B, S, D), np.float32)
    for c in range(8):
        b, j = divmod(c, 4)
        outp[b, j * TOKC:(j + 1) * TOKC] = res.results[c]["out"]
    return outp
```

```python
from contextlib import ExitStack

import ml_dtypes
import numpy as np

import concourse.bass as bass
import concourse.mybir as mybir
from concourse.bass_utils import run_bass_kernel_spmd

F32 = mybir.dt.float32
BF16 = mybir.dt.bfloat16
AF = mybir.ActivationFunctionType
ALU = mybir.AluOpType
AX = mybir.AxisListType

D = 1024
DFF = 2816
NM = DFF // 128
EPS = 1e-5
LAM_INIT = 0.8 - 0.6 * 1.0
NEGM = -30000.0
ENGS = ("pe", "act", "dve", "pool", "sp")
C_DQ, C_DQS, C_DK, C_DKS, C_FQ0, C_FQ1, C_FK0, C_FK1, C_V, NCOL = 0, 128, 256, 384, 512, 576, 640, 710, 780, 1036


class Sem:
    def __init__(self, h, eng):
        self.h = h
        self.eng = eng
        self.n = 0


class Prog:
    def __init__(self, nc, stack):
        self.nc = nc
        self.stack = stack
        self.esem = {e: Sem(stack.enter_context(nc.semaphore("s_" + e)), e) for e in ("pe", "act", "dve", "pool")}
        self.all_sems = list(self.esem.values())
        self.waited = {e: {} for e in ENGS}
        self.q = {e: [] for e in ENGS}
        self.lastw = {}
        self.readers = {}

    def new_sem(self, name):
        s = Sem(self.stack.enter_context(self.nc.semaphore(name)), None)
        self.all_sems.append(s)
        return s

    def _emit_waits(self, eng, evs):
        w = self.waited[eng]
        for s, v in evs:
            if w.get(id(s), 0) < v:
                w[id(s)] = v
                self.q[eng].append(lambda e, h=s.h, v=v: e.wait_ge(h, v))

    def _deps(self, eng, reads, writes):
        evs = []
        for k in reads:
            ev = self.lastw.get(k)
            if ev is not None and not (eng == "pe" and ev[0].eng == "pe"):
                evs.append(ev)
        for k in writes:
            ev = self.lastw.get(k)
            if ev is not None and not (eng == "pe" and ev[0].eng == "pe"):
                evs.append(ev)
            for ev in self.readers.get(k, ()):
                if not (eng == "pe" and ev[0].eng == "pe"):
                    evs.append(ev)
        return evs

    def _record(self, ev, reads, writes):
        for k in writes:
            self.lastw[k] = ev
            self.readers[k] = []
        for k in reads:
            self.readers.setdefault(k, []).append(ev)

    def op(self, eng, fn, reads=(), writes=(), signal=True):
        self._emit_waits(eng, self._deps(eng, reads, writes))
        if signal:
            s = self.esem[eng]
            s.n += 1
            self.q[eng].append(lambda e, fn=fn, h=s.h: fn(e).then_inc(h, 1))
            self._record((s, s.n), reads, writes)
        else:
            self.q[eng].append(lambda e, fn=fn: fn(e))

    def group(self, eng, fns, reads=(), writes=()):
        self._emit_waits(eng, self._deps(eng, reads, writes))
        for fn in fns[:-1]:
            self.q[eng].append(lambda e, fn=fn: fn(e))
        s = self.esem[eng]
        s.n += 1
        self.q[eng].append(lambda e, fn=fns[-1], h=s.h: fn(e).then_inc(h, 1))
        self._record((s, s.n), reads, writes)

    def dma(self, eng, out, in_, sem, reads=(), writes=(), **kw):
        self._emit_waits(eng, self._deps(eng, reads, writes))
        sem.n += 16
        self.q[eng].append(lambda e, o=out, i=in_, h=sem.h, kw=kw: e.dma_start(out=o, in_=i, **kw).then_inc(h, 16))
        self._record((sem, sem.n), reads, writes)

    def cc(self, fn, sem, reads=(), writes=()):
        self._emit_waits("pool", self._deps("pool", reads, ("cc_chain",)))
        sem.n += 1
        self.q["pool"].append(lambda e, fn=fn, h=sem.h: fn(e).then_inc(h, 1))
        self._record((sem, sem.n), reads, ("cc_chain",) + tuple(writes))

    def raw(self, eng, fn):
        self.q[eng].append(fn)

    def barrier(self, exclude=(), keep_prefix=None):
        ex = {id(s) for s in exclude}
        evs = [(s, s.n) for s in self.all_sems if s.n > 0 and id(s) not in ex]
        for e in ENGS:
            self._emit_waits(e, evs)
        keep = {k: v for k, v in self.lastw.items() if keep_prefix and k.startswith(keep_prefix)}
        self.lastw.clear()
        self.readers.clear()
        self.lastw.update(keep)

    def emit(self):
        q = self.q
        with self.nc.Block() as blk:
            @blk.tensor
            def _(e):
                for f in q["pe"]:
                    f(e)

            @blk.scalar
            def _(e):
                for f in q["act"]:
                    f(e)

            @blk.vector
            def _(e):
                for f in q["dve"]:
                    f(e)

            @blk.gpsimd
            def _(e):
                for f in q["pool"]:
                    f(e)

            @blk.sync
            def _(e):
                for f in q["sp"]:
                    f(e)
        self.q = {e: [] for e in ENGS}


def build(S, debug=False):
    NT = S // 512
    TOKC = S // 4
    NSUB4 = TOKC // 128
    nc = bass.Bass("TRN2", target_bir_lowering=False)

    def din(name, shape, dt=F32):
        return nc.dram_tensor(name, shape, dt, kind="ExternalInput").ap()

    xb = din("xb", [S, D])
    xres = din("xres", [TOKC, D])
    w_in = din("w_in", [D, NCOL])
    g1T = din("g1T", [128, 8])
    g2T = din("g2T", [128, 8])
    bfp = din("bfp", [128, 2])
    lamv = din("lamv", [128, 256])
    sgv = din("sgv", [128, 1])
    pcv = din("pcv", [128, 6])
    ohv = din("ohv", [128, 4])
    gfb = din("gfb", [128, D])
    ropeC = din("ropeC", [128, S])
    ropeS = din("ropeS", [128, S])
    cbf = din("cbf", [128, 384], BF16)
    swfd = din("swfd", [128, 128])
    w_out = din("w_out", [D, D])
    w_gate = din("w_gate", [D, DFF])
    w_up = din("w_up", [D, DFF])
    w_down = din("w_down", [DFF, D])
    out = nc.dram_tensor("out", [TOKC, D], F32, kind="ExternalOutput").ap()
    CH = min(1024, S)
    NG = S // CH
    mix_in = [nc.dram_tensor(f"mix_in{g}", [256, CH], BF16) for g in range(NG)]
    mix_all = [nc.dram_tensor(f"mix_all{g}", [1024, CH], BF16) for g in range(NG)]
    CW = [768, 640, 768, 640]
    CS = [0, 768, 1408, 2176]
    WoS = nc.dram_tensor("WoS", [128, 8, D], BF16)
    WgS = [nc.dram_tensor(f"WgS{c}", [128, 8, CW[c]], BF16) for c in range(4)]
    WuS = [nc.dram_tensor(f"WuS{c}", [128, 8, CW[c]], BF16) for c in range(4)]
    WdS = nc.dram_tensor("WdS", [128, NM, D], BF16)
    dbg_mix = nc.dram_tensor("dbg_mix", [256, S], BF16, kind="ExternalOutput").ap() if debug else None

    with ExitStack() as gst:
        P = Prog(nc, gst)
        sb = lambda st, name, shape, dt: st.enter_context(nc.sbuf_tensor(name, shape, dt))
        ps = lambda st, name, shape, dt=F32: st.enter_context(nc.psum_tensor(name, shape, dt))

        cb = sb(gst, "cb", [128, 384], BF16)
        ident, maskc, maskd = cb[:, 0:128], cb[:, 128:256], cb[:, 256:384]
        onesb = sb(gst, "onesb", [128, 128], BF16)
        onesf = sb(gst, "onesf", [128, 128], F32)
        swf = sb(gst, "swf", [128, 128], F32)
        g1s = sb(gst, "g1s", [128, 8], F32)
        g2s = sb(gst, "g2s", [128, 8], F32)
        negb = sb(gst, "negb", [128, 2], F32)
        lams = sb(gst, "lams", [128, 256], F32)
        lamp = sb(gst, "lamp", [128, 128], F32)
        lsc = sb(gst, "lsc", [128, 8], F32)
        pc = sb(gst, "pc", [128, 6], F32)
        oh = sb(gst, "oh", [128, 4], F32)
        neglam = lsc[:, 4:5]
        sg = lsc[:, 5:6]
        csem = P.new_sem("csem")
        ccsems = [P.new_sem(f"ccsem{g}") for g in range(NG)]

        WA = sb(gst, "WA", [128, 32768], BF16)
        QdT = WA[:, 0:S]
        KdT = WA[:, 8192:8192 + S]
        Wo = WA[:, 0:8192].rearrange("p (k c) -> p k c", k=8)
        WgL, WuL = [None] * 4, [None] * 4
        WgL[0] = WA[:, 8192:14336].rearrange("p (k c) -> p k c", k=8)
        WuL[0] = WA[:, 14336:20480].rearrange("p (k c) -> p k c", k=8)
        WgL[1] = WA[:, 20480:25600].rearrange("p (k c) -> p k c", k=8)
        WuL[1] = WA[:, 25600:30720].rearrange("p (k c) -> p k c", k=8)
        with ExitStack() as ast:
            QfT = [WA[:, 16384:16384 + S], sb(ast, "QfT1", [128, S], BF16)]
            KfT = [WA[:, 24576:24576 + S], sb(ast, "KfT1", [128, S], BF16)]
            Vall = sb(ast, "Vall", [128, S // 128, 256], BF16)

            with ExitStack() as p1st:
                w_bf = sb(p1st, "w_bf", [128, 8, NCOL], BF16)
                with ExitStack() as st:
                    NSTG = 4
                    stg = [sb(st, f"stg{i}", [128, NCOL], F32) for i in range(NSTG)]
                    ssem = [P.new_sem(f"stg_s{i}") for i in range(NSTG)]
                    for (dst, src) in ((cb, cbf), (g1s, g1T), (g2s, g2T), (negb, bfp), (lams, lamv), (pc, pcv), (oh, ohv), (swf, swfd)):
                        P.dma("sp", dst[:, :], src[:, :], csem, writes=("consts",))
                    P.dma("sp", lsc[:, 5:6], sgv[:, :], csem, writes=("consts",))
                    P.op("dve", lambda e: e.memset(onesb[:, :], 1.0), writes=("onesb",))
                    P.op("dve", lambda e: e.memset(onesf[:, :], 1.0), writes=("onesf",))
                    P.op("dve", lambda e: e.tensor_scalar(out=negb[64:70, :], in0=negb[64:70, :], scalar1=-1.0, scalar2=None, op0=ALU.mult),
                         reads=("consts",), writes=("negb",))
                    P.op("dve", lambda e: e.tensor_scalar(out=lsc[:, 5:6], in0=lsc[:, 5:6], scalar1=1.0 - LAM_INIT, scalar2=None, op0=ALU.mult),
                         reads=("consts",), writes=("sg",))
                    P.op("dve", lambda e: e.tensor_tensor(out=lamp[:, 0:64], in0=lams[:, 0:64], in1=lams[:, 64:128], op=ALU.mult),
                         reads=("consts",), writes=("lamp0",))
                    P.op("dve", lambda e: e.tensor_tensor(out=lamp[:, 64:128], in0=lams[:, 128:192], in1=lams[:, 192:256], op=ALU.mult),
                         reads=("consts",), writes=("lamp1",))
                    P.op("dve", lambda e: e.reduce_sum(out=lsc[:, 0:1], in_=lamp[:, 0:64], axis=AX.X), reads=("lamp0",), writes=("ls0",))
                    P.op("dve", lambda e: e.reduce_sum(out=lsc[:, 1:2], in_=lamp[:, 64:128], axis=AX.X), reads=("lamp1",), writes=("ls1",))
                    P.op("act", lambda e: e.activation(out=lsc[:, 2:4], in_=lsc[:, 0:2], func=AF.Exp), reads=("ls0", "ls1"), writes=("le",))
                    P.op("dve", lambda e: e.tensor_tensor(out=lsc[:, 4:5], in0=lsc[:, 3:4], in1=lsc[:, 2:3], op=ALU.subtract),
                         reads=("le",), writes=("nl0",))
                    P.op("dve", lambda e: e.tensor_scalar(out=lsc[:, 4:5], in0=lsc[:, 4:5], scalar1=-LAM_INIT, scalar2=None, op0=ALU.add),
                         reads=("nl0",), writes=("neglam",))
                    for kc in range(8):
                        sl = kc % NSTG
                        P.dma("act" if kc % 2 == 0 else "sp", stg[sl][:, :], w_in[kc * 128:(kc + 1) * 128, :], ssem[sl], writes=(f"stg{sl}",))
                        eng = "dve"
                        P.op(eng, lambda e, kc=kc, sl=sl: e.tensor_scalar(out=w_bf[:, kc, :], in0=stg[sl][:, :], scalar1=g1s[:, kc:kc + 1],
                                                                           scalar2=None, op0=ALU.mult),
                             reads=(f"stg{sl}", "consts"), writes=("w_bf",))
                    P.barrier()
                    P.emit()

                with ExitStack() as st:
                    xsb = [sb(st, f"xsb{i}", [128, D], F32) for i in range(2)]
                    xsem = [P.new_sem(f"xsem{i}") for i in range(2)]
                    xn = [sb(st, f"xn{i}", [128, D], BF16) for i in range(2)]
                    hT = [sb(st, f"hT{i}", [128, 8, 512], BF16) for i in range(2)]
                    cs = [sb(st, f"cs{i}", [128, 2, 512], F32) for i in range(2)]
                    cssem = [P.new_sem(f"cssem{i}") for i in range(2)]
                    t1 = sb(st, "t1", [128, 512], F32)
                    t2 = sb(st, "t2", [128, 512], F32)
                    zeros = sb(st, "zeros", [128, 512], F32)
                    spt = sb(st, "spt", [128, 512], F32)
                    cum = [[sb(st, f"cum{h}_{i}", [128, 512], F32) for i in range(2)] for h in range(2)]
                    hib = sb(st, "hib", [128, 512], BF16)
                    r1n = sb(st, "r1n", [128, 512], F32)
                    mnb = sb(st, "mnb", [128, 512], BF16)
                    r2 = r1n
                    stat = sb(st, "stat", [128, 8], F32)
                    ptr = [ps(st, f"ptr{i}", [128, 8, 128], BF16) for i in range(2)]
                    pj = [ps(st, f"pj{i}", [128, 512]) for i in range(6)]
                    pjn = [0]

                    def nextpj():
                        i = pjn[0] % 6
                        pjn[0] += 1
                        return pj[i], f"pj{i}"

                    P.op("pool", lambda e: e.memset(zeros[:, :], 0.0), writes=("zeros",))

                    def stage_a(t):
                        sl = t % 2
                        sc = 3 * (t % 2)
                        P.dma("sp", xsb[sl][:, :], xb[t * 128:(t + 1) * 128, :], xsem[sl], writes=(f"xsb{sl}",))
                        P.op("act", lambda e: e.activation(out=xn[sl][:, :], in_=xsb[sl][:, :], func=AF.Square, accum_out=stat[:, sc:sc + 1]),
                             reads=(f"xsb{sl}",), writes=(f"xn{sl}", f"ss{sl}"))
                        P.op("act", lambda e: e.activation(out=stat[:, sc + 1:sc + 2], in_=stat[:, sc:sc + 1], func=AF.Ln, scale=1.0 / D, bias=EPS),
                             reads=(f"ss{sl}",), writes=(f"lnv{sl}",))
                        P.op("act", lambda e: e.activation(out=stat[:, sc + 2:sc + 3], in_=stat[:, sc + 1:sc + 2], func=AF.Exp, scale=-0.5),
                             reads=(f"lnv{sl}",), writes=(f"rstd{sl}",))
                        P.op("dve", lambda e: e.tensor_scalar(out=xn[sl][:, :], in0=xsb[sl][:, :], scalar1=stat[:, sc + 2:sc + 3],
                                                              scalar2=None, op0=ALU.mult),
                             reads=(f"xsb{sl}", f"rstd{sl}"), writes=(f"xn{sl}",))

                    def stage_b(t):
                        sl = t % 2
                        sub = t % 4
                        hs = (t // 4) % 2
                        pt_ = ptr[t % 2]
                        fns = [(lambda e, kc=kc: e.transpose(out=pt_[:, kc, :], in_=xn[sl][:, kc * 128:(kc + 1) * 128], identity=ident)) for kc in range(8)]
                        P.group("pe", fns, reads=(f"xn{sl}",), writes=(f"ptr{t % 2}",))
                        if sub % 2 == 0:
                            P.op("act", lambda e: e.activation(out=hT[hs][:, :, sub * 128:(sub + 1) * 128], in_=pt_[:, :, :], func=AF.Copy),
                                 reads=(f"ptr{t % 2}",), writes=(f"hT{hs}_{sub}",))
                        else:
                            P.op("dve", lambda e: e.tensor_copy(out=hT[hs][:, :, sub * 128:(sub + 1) * 128], in_=pt_[:, :, :]),
                                 reads=(f"ptr{t % 2}",), writes=(f"hT{hs}_{sub}",))

                    def stage_c(i):
                        c0 = i * 512
                        csl = i % 2
                        hs = i % 2
                        hcur = hT[hs]
                        hkeys = tuple(f"hT{hs}_{s_}" for s_ in range(4))

                        def proj_fm(col0, M):
                            pt, key = nextpj()
                            fns = [(lambda e, kc=kc: e.matmul(pt[0:M, :], lhsT=w_bf[:, kc, col0:col0 + M], rhs=hcur[:, kc, :],
                                                              start=(kc == 0), stop=(kc == 7))) for kc in range(8)]
                            P.group("pe", fns, reads=hkeys, writes=(key,))
                            return pt, key

                        def vpart(sub):
                            t = 4 * i + sub
                            pt, key = nextpj()
                            fns = [(lambda e, kc=kc: e.matmul(pt[:, 0:256], lhsT=hcur[:, kc, sub * 128:(sub + 1) * 128],
                                                              rhs=w_bf[:, kc, C_V:C_V + 256], start=(kc == 0), stop=(kc == 7))) for kc in range(8)]
                            P.group("pe", fns, reads=(f"hT{hs}_{sub}",), writes=(key,))
                            P.op("act", lambda e: e.activation(out=Vall[:, t, :], in_=pt[:, 0:256], func=AF.Copy), reads=(key,))

                        def rope(cq, cqs, dst):
                            pa, ka = proj_fm(cq, 128)
                            pb_, kb = proj_fm(cqs, 128)
                            P.op("dve", lambda e: e.tensor_tensor(out=t1[:, :], in0=pa[:, :], in1=cs[csl][:, 0, :], op=ALU.mult),
                                 reads=(ka, f"cs{csl}"), writes=("t1",))
                            P.op("dve", lambda e: e.tensor_tensor(out=t2[:, :], in0=pb_[:, :], in1=cs[csl][:, 1, :], op=ALU.mult),
                                 reads=(kb, f"cs{csl}"), writes=("t2",))
                            P.op("pool", lambda e: e.tensor_tensor(out=dst[:, c0:c0 + 512], in0=t1[:, :], in1=t2[:, :], op=ALU.add),
                                 reads=("t1", "t2"))

                        def foxq(h):
                            pa, ka = proj_fm(C_FQ0 + 64 * h, 128)
                            P.op("act", lambda e: e.activation(out=QfT[h][0:64, c0:c0 + 512], in_=pa[0:64, :], func=AF.Copy, scale=0.125), reads=(ka,))

                        def foxk(h):
                            pa, ka = proj_fm(C_FK0 + 70 * h, 128)
                            P.op("act", lambda e: e.activation(out=KfT[h][0:64, c0:c0 + 512], in_=pa[0:64, :], func=AF.Copy), reads=(ka,))
                            R = slice(64, 70)
                            P.op("act", lambda e: e.activation(out=spt[R, :], in_=pa[R, :], func=AF.Exp, scale=-1.0, bias=negb[R, h:h + 1]),
                                 reads=(ka, "negb"), writes=("spt",))
                            P.op("act", lambda e: e.activation(out=spt[R, :], in_=spt[R, :], func=AF.Ln, bias=1.0), reads=("spt",), writes=("spt",))
                            cur, prev = cum[h][i % 2], cum[h][(i + 1) % 2]
                            init = 0.0 if i == 0 else prev[R, 511:512]
                            ck = f"cum{h}_{i % 2}"
                            P.op("dve", lambda e: e.tensor_tensor_scan(out=cur[R, :], data0=spt[R, :], data1=zeros[R, :], initial=init,
                                                                       op0=ALU.add, op1=ALU.add),
                                 reads=("spt", "zeros", f"cum{h}_{(i + 1) % 2}"), writes=(ck,))
                            P.op("pool", lambda e: e.tensor_copy(out=hib[R, :], in_=cur[R, :]), reads=(ck,), writes=("hib",))
                            P.op("dve", lambda e: e.scalar_tensor_tensor(out=r1n[R, :], in0=hib[R, :], scalar=pc[R, 0:1], in1=cur[R, :],
                                                                         op0=ALU.mult, op1=ALU.subtract), reads=("hib", ck), writes=("r1n",))
                            P.op("pool", lambda e: e.tensor_copy(out=mnb[R, :], in_=r1n[R, :]), reads=("r1n",), writes=("mnb",))
                            P.op("dve", lambda e: e.scalar_tensor_tensor(out=r2[R, :], in0=mnb[R, :], scalar=pc[R, 1:2], in1=r1n[R, :],
                                                                         op0=ALU.mult, op1=ALU.subtract), reads=("mnb", "r1n"), writes=("r1n",))
                            P.op("dve", lambda e: e.tensor_scalar(out=KfT[h][R, c0:c0 + 512], in0=r2[R, :], scalar1=pc[R, 2:3], scalar2=pc[R, 3:4],
                                                                  op0=ALU.mult, op1=ALU.add), reads=("r1n",))
                            P.op("dve", lambda e: e.tensor_scalar(out=QfT[h][R, c0:c0 + 512], in0=r2[R, :], scalar1=pc[R, 4:5], scalar2=pc[R, 5:6],
                                                                  op0=ALU.mult, op1=ALU.add), reads=("r1n",))

                        def part0():
                            P.dma("pool", cs[csl][:, 0, :], ropeC[:, c0:c0 + 512], cssem[csl], writes=(f"cs{csl}",))
                            P.dma("pool", cs[csl][:, 1, :], ropeS[:, c0:c0 + 512], cssem[csl], writes=(f"cs{csl}",))
                            foxk(0)
                            vpart(0)
                            rope(C_DQ, C_DQS, QdT)

                        def part1():
                            foxk(1)
                            vpart(1)
                            rope(C_DK, C_DKS, KdT)

                        def part2():
                            vpart(2)
                            foxq(0)

                        def part3():
                            vpart(3)
                            foxq(1)

                        return [part0, part1, part2, part3]

                    stage_a(0)
                    stage_a(1)
                    stage_b(0)
                    stage_a(2)
                    stage_b(1)
                    stage_a(3)
                    stage_b(2)
                    stage_b(3)
                    for i in range(NT):
                        parts = stage_c(i)
                        for sub in range(4):
                            if i + 1 < NT:
                                stage_a(4 * (i + 1) + sub)
                            parts[sub]()
                            if i + 1 < NT and sub >= 1:
                                stage_b(4 * (i + 1) + sub - 1)
                        if i + 1 < NT:
                            stage_b(4 * (i + 1) + 3)
                    P.barrier()
                    P.emit()

            with ExitStack() as st:
                NB = 3
                pt1 = [sb(st, f"pt1_{i}", [128, 512], BF16) for i in range(NB)]
                pt2 = [sb(st, f"pt2_{i}", [128, 512], BF16) for i in range(NB)]
                ft = [sb(st, f"ft{i}", [128, 512], F32) for i in range(6)]
                mo = [sb(st, f"mo{i}", [128, 512], BF16) for i in range(2)]
                mosem = [P.new_sem(f"mosem{i}") for i in range(2)]
                bank = [ps(st, f"bk{i}", [128, 512]) for i in range(8)]
                VF = sb(st, "VF", [128, S // 128, 192], BF16)
                P.op("pool", lambda e: e.tensor_copy(out=VF[:, :, 0:64], in_=Vall[:, :, 128:192]), writes=("VF0",))
                P.op("pool", lambda e: e.tensor_copy(out=VF[:, :, 128:192], in_=Vall[:, :, 192:256]), writes=("VF1",))
                P.op("pool", lambda e: e.memset(VF[:, :, 64:128], 1.0), writes=("VF2",))
                O1, O2, S1, S2 = bank[0], bank[1], bank[2], bank[3]
                s1b, s2b = [bank[4], bank[5]], [bank[6], bank[7]]
                pstg = [sb(st, f"pstg{i}", [128, D], F32) for i in range(2)]
                pcb = [sb(st, f"pcb{i}", [128, D], BF16) for i in range(2)]
                pisem = [P.new_sem(f"pisem{i}") for i in range(2)]
                posem = [P.new_sem(f"posem{i}") for i in range(2)]
                pjobs = []
                for fc in range(8):
                    pjobs.append((w_out[fc * 128:(fc + 1) * 128, :], WoS[:, fc, :], D, (sg if fc % 2 == 0 else None)))
                for c in range(4):
                    for kc in range(8):
                        pjobs.append((w_gate[kc * 128:(kc + 1) * 128, CS[c]:CS[c] + CW[c]], WgS[c][:, kc, :], CW[c], g2s[:, kc:kc + 1]))
                        pjobs.append((w_up[kc * 128:(kc + 1) * 128, CS[c]:CS[c] + CW[c]], WuS[c][:, kc, :], CW[c], g2s[:, kc:kc + 1]))
                NPRE = 8 + 32
                for m in range(NM):
                    pjobs.append((w_down[m * 128:(m + 1) * 128, :], WdS[:, m, :], D, None))
                pstate = {"n": 0, "iters": 0}
                wpsem = [P.new_sem(f"wpsem{i}") for i in range(5)]
                TOT_IT = 3 * sum(4 * Q_ + 4 for Q_ in range(NT))
                PER = max(1, (TOT_IT - 8) // (len(pjobs) + 2))

                def pre_in(n):
                    src, dst, wd, gain = pjobs[n]
                    sl = n % 2
                    P.dma("sp", pstg[sl][:, 0:wd], src, pisem[sl], writes=(f"pstg{sl}",))

                def pre_cast_out(n):
                    src, dst, wd, gain = pjobs[n]
                    sl = n % 2
                    eng = "pool"
                    if gain is None:
                        P.op(eng, lambda e: e.tensor_copy(out=pcb[sl][:, 0:wd], in_=pstg[sl][:, 0:wd]), reads=(f"pstg{sl}",), writes=(f"pcb{sl}",))
                    else:
                        P.op(eng, lambda e: e.tensor_scalar(out=pcb[sl][:, 0:wd], in0=pstg[sl][:, 0:wd], scalar1=gain, scalar2=None, op0=ALU.mult),
                             reads=(f"pstg{sl}",), writes=(f"pcb{sl}",))
                    P.dma("sp", dst, pcb[sl][:, 0:wd], posem[sl], reads=(f"pcb{sl}",))

                def pre_step():
                    n = pstate["n"]
                    if n <= len(pjobs):
                        if n < len(pjobs):
                            pre_in(n)
                        if n >= 1:
                            pre_cast_out(n - 1)
                        pstate["n"] = n + 1

                T_DIFF, T_FOX = 1.84, 0.64
                n_it = sum(4 * Q_ + 4 for Q_ in range(NT))
                PACE = 0.92 * (n_it * T_DIFF + 2 * n_it * T_FOX) / (len(pjobs) + 1)
                pstate["acc"] = 0.0

                def pre_tick(w=T_DIFF):
                    pstate["acc"] += w
                    if pstate["acc"] >= PACE:
                        pstate["acc"] -= PACE
                        pre_step()

                def tiles_of(Q):
                    return [(kt, max(0, kt - 4 * Q)) for kt in range(4 * Q + 4)]

                it = 0
                pend = {"b": None}
                for Q in range(NT):
                    q0 = Q * 512
                    tl = tiles_of(Q)
                    last = len(tl) - 1

                    def scores(idx, it, tl=tl, q0=q0):
                        kt, n0 = tl[idx]
                        sl = it % 2
                        diag = kt >= (q0 // 128)
                        for (sbk, lo, key) in ((s1b[sl], 0, f"s1_{sl}"), (s2b[sl], 64, f"s2_{sl}")):
                            fns = [lambda e, sbk=sbk, lo=lo, kt=kt, n0=n0: e.matmul(sbk[:, n0 * 128:512], lhsT=KdT[lo:lo + 64, kt * 128:(kt + 1) * 128],
                                                                                     rhs=QdT[lo:lo + 64, q0 + n0 * 128:q0 + 512], start=True, stop=not diag)]
                            if diag:
                                fns.append(lambda e, sbk=sbk, n0=n0: e.matmul(sbk[:, n0 * 128:(n0 + 1) * 128], lhsT=ident, rhs=maskd, start=False, stop=True))
                            P.group("pe", fns, writes=(key,))

                    def exps(idx, it, tl=tl):
                        kt, n0 = tl[idx]
                        sl, bl = it % 2, it % NB
                        P.op("act", lambda e: e.activation(out=pt1[bl][:, n0 * 128:512], in_=s1b[sl][:, n0 * 128:512], func=AF.Exp, scale=0.125),
                             reads=(f"s1_{sl}",), writes=(f"pt1_{bl}",))
                        P.op("act", lambda e: e.activation(out=pt2[bl][:, n0 * 128:512], in_=s2b[sl][:, n0 * 128:512], func=AF.Exp, scale=0.125),
                             reads=(f"s2_{sl}",), writes=(f"pt2_{bl}",))

                    def pv(idx, it, tl=tl, last=last):
                        kt, n0 = tl[idx]
                        bl = it % NB
                        st_, sp_ = (idx == 0), (idx == last)
                        cs_ = slice(n0 * 128, 512)
                        fns = [
                            lambda e: e.matmul(O1[:, cs_], lhsT=Vall[:, kt, 0:128], rhs=pt1[bl][:, cs_], start=st_, stop=sp_),
                            lambda e: e.matmul(S1[:, cs_], lhsT=onesb[:, :], rhs=pt1[bl][:, cs_], start=st_, stop=sp_),
                            lambda e: e.matmul(O2[:, cs_], lhsT=Vall[:, kt, 0:128], rhs=pt2[bl][:, cs_], start=st_, stop=sp_),
                            lambda e: e.matmul(S2[:, cs_], lhsT=onesb[:, :], rhs=pt2[bl][:, cs_], start=st_, stop=sp_),
                        ]
                        P.group("pe", fns, reads=(f"pt1_{bl}", f"pt2_{bl}"), writes=("O1", "O2", "S1", "S2"))

                    for idx in range(len(tl)):
                        scores(idx, it + idx)
                        exps(idx, it + idx)
                        if idx >= 1:
                            pv(idx - 1, it + idx - 1)
                        if idx == min(7, last) and pend["b"] is not None:
                            pend["b"]()
                            pend["b"] = None
                        pre_tick()
                    pv(last, it + last)
                    it += len(tl)
                    r1_, o1_, r2_, o2_, od_, sq_ = ft
                    P.op("dve", lambda e: e.tensor_copy(out=r1_[:, :], in_=S1[:, :]), reads=("S1",), writes=("r1",))
                    P.op("dve", lambda e: e.tensor_copy(out=o1_[:, :], in_=O1[:, :]), reads=("O1",), writes=("o1",))
                    P.op("dve", lambda e: e.tensor_copy(out=r2_[:, :], in_=S2[:, :]), reads=("S2",), writes=("r2",))
                    P.op("dve", lambda e: e.tensor_copy(out=o2_[:, :], in_=O2[:, :]), reads=("O2",), writes=("o2",))
                    P.op("dve", lambda e: e.reciprocal(out=r1_[:, :], in_=r1_[:, :]), reads=("r1",), writes=("r1",))
                    P.op("dve", lambda e: e.tensor_tensor(out=o1_[:, :], in0=o1_[:, :], in1=r1_[:, :], op=ALU.mult), reads=("o1", "r1"), writes=("o1",))
                    P.op("dve", lambda e: e.reciprocal(out=r2_[:, :], in_=r2_[:, :]), reads=("r2",), writes=("r2",))
                    P.op("dve", lambda e: e.tensor_tensor(out=o2_[:, :], in0=o2_[:, :], in1=r2_[:, :], op=ALU.mult), reads=("o2", "r2"), writes=("o2",))
                    P.op("dve", lambda e: e.scalar_tensor_tensor(out=od_[:, :], in0=o2_[:, :], scalar=neglam, in1=o1_[:, :], op0=ALU.mult, op1=ALU.add),
                         reads=("o1", "o2"), writes=("od",))
                    P.op("dve", lambda e: e.tensor_tensor(out=sq_[:, :], in0=od_[:, :], in1=od_[:, :], op=ALU.mult), reads=("od",), writes=("sq",))

                    def part_b(Q=Q, q0=q0):
                        ssq = s1b[0]
                        P.group("pe", [lambda e: e.matmul(ssq[:, :], lhsT=onesf[:, :], rhs=sq_[:, :], start=True, stop=True)], reads=("sq",), writes=("s1_0",))
                        P.op("act", lambda e: e.activation(out=r1_[:, :], in_=ssq[:, :], func=AF.Ln, scale=1.0 / 128, bias=EPS), reads=("s1_0",), writes=("r1",))
                        P.op("act", lambda e: e.activation(out=r2_[:, :], in_=r1_[:, :], func=AF.Exp, scale=-0.5), reads=("r1",), writes=("r2",))
                        ms = Q % 2
                        P.op("dve", lambda e: e.tensor_tensor(out=mo[ms][:, :], in0=od_[:, :], in1=r2_[:, :], op=ALU.mult),
                             reads=("od", "r2"), writes=(f"mo{ms}",))
                        P.dma("sp", mix_in[q0 // CH][0:128, q0 % CH:q0 % CH + 512], mo[ms][:, :], mosem[ms], reads=(f"mo{ms}",))

                    pend["b"] = part_b
                if pend["b"] is not None:
                    pend["b"]()
                    pend["b"] = None
                P.barrier()
                Of = [bank[0], bank[1]]
                SWo = [bank[2], bank[3]]
                sfb = [bank[4], bank[5], bank[6], bank[7]]
                for k_ in range(4):
                    P.op("pool", lambda e, k_=k_: e.memset(ft[k_][:, :], 0.0), writes=(f"Rt{k_}",))
                it = 0
                ftail = []
                for h in range(2):
                    vlo = 128 + 64 * h
                    vf0 = 64 * h
                    orows = slice(0, 64) if h == 0 else slice(64, 128)
                    srows = slice(64, 128) if h == 0 else slice(0, 64)
                    for Q in range(NT):
                        q0 = Q * 512
                        tl = tiles_of(Q)
                        last = len(tl) - 1
                        ab = (h * NT + Q) % 2

                        def scores(idx, it, tl=tl, q0=q0, h=h):
                            kt, n0 = tl[idx]
                            sl = it % 4
                            diag = kt >= (q0 // 128)
                            sbk = sfb[sl]
                            fns = [lambda e: e.matmul(sbk[:, n0 * 128:512], lhsT=KfT[h][0:70, kt * 128:(kt + 1) * 128],
                                                      rhs=QfT[h][0:70, q0 + n0 * 128:q0 + 512], start=True, stop=not diag)]
                            if diag:
                                fns.append(lambda e: e.matmul(sbk[:, n0 * 128:(n0 + 1) * 128], lhsT=ident, rhs=maskc, start=False, stop=True))
                            P.group("pe", fns, writes=(f"sf_{sl}",))

                        def exps(idx, it, tl=tl):
                            kt, n0 = tl[idx]
                            sl, bl = it % 4, it % (2 * NB)
                            ptile = (pt1 + pt2)[bl]
                            P.op("act", lambda e: e.activation(out=ptile[:, n0 * 128:512], in_=sfb[sl][:, n0 * 128:512], func=AF.Exp),
                                 reads=(f"sf_{sl}",), writes=(f"pf_{bl}",))

                        def pv(idx, it, tl=tl, last=last, ab=ab, vf0=vf0):
                            kt, n0 = tl[idx]
                            bl = it % (2 * NB)
                            ptile = (pt1 + pt2)[bl]
                            st_, sp_ = (idx == 0), (idx == last)
                            cs_ = slice(n0 * 128, 512)
                            fns = [lambda e: e.matmul(Of[ab][:, cs_], lhsT=VF[:, kt, vf0:vf0 + 128], rhs=ptile[:, cs_], start=st_, stop=sp_)]
                            P.group("pe", fns, reads=(f"pf_{bl}", "VF0", "VF1", "VF2"), writes=(f"Of{ab}",))

                        for idx in range(len(tl)):
                            scores(idx, it + idx)
                            exps(idx, it + idx)
                            if idx >= 2:
                                pv(idx - 2, it + idx - 2)
                            elif ftail:
                                ftail.pop(0)()
                            if idx == 3 and pend["b"] is not None:
                                pend["b"]()
                                pend["b"] = None
                            pre_tick(T_FOX)
                        Rt = ft[2 * h + ab]
                        rb = ft[4 + ab]
                        ms = Q % 2

                        def tail0(pv=pv, it0=it, last=last):
                            pv(last - 1, it0 + last - 1)

                        def tail1(pv=pv, it0=it, last=last, Rt=Rt, ab=ab, srows=srows, h=h):
                            pv(last, it0 + last)
                            P.op("dve", lambda e: e.reciprocal(out=Rt[srows, :], in_=Of[ab][srows, :]),
                                 reads=(f"Of{ab}",), writes=(f"Rt{2 * h + ab}",))

                        it += len(tl)

                        def part_b(Rt=Rt, rb=rb, ab=ab, ms=ms, orows=orows, h=h, Q=Q, q0=q0, vlo=vlo):
                            P.group("pe", [lambda e: e.matmul(SWo[ab][:, :], lhsT=swf[:, :], rhs=Rt[:, :], start=True, stop=True)],
                                    reads=(f"Rt{2 * h + ab}",), writes=(f"SWo{ab}",))
                            P.op("act", lambda e: e.activation(out=rb[orows, :], in_=SWo[ab][orows, :], func=AF.Copy),
                                 reads=(f"SWo{ab}",), writes=(f"rb{ab}",))
                            P.op("dve", lambda e: e.tensor_tensor(out=mo[ms][orows, :], in0=Of[ab][orows, :], in1=rb[orows, :], op=ALU.mult),
                                 reads=(f"Of{ab}", f"rb{ab}"), writes=(f"mo{ms}",))
                            P.dma("sp", mix_in[q0 // CH][vlo:vlo + 64, q0 % CH:q0 % CH + 512], mo[ms][orows, :], mosem[ms], reads=(f"mo{ms}",),
                                  writes=(f"mixin_{h}_{Q}",))
                            if h == 1 and (q0 + 512) % CH == 0:
                                g = q0 // CH
                                deps = tuple(f"mixin_{hh}_{QQ}" for hh in range(2) for QQ in range(g * CH // 512, (g + 1) * CH // 512))
                                P.cc(lambda e: e.collective_compute("AllGather", ALU.bypass, replica_groups=[[0, 1, 2, 3], [4, 5, 6, 7]],
                                                                    ins=[mix_in[g].ap().opt()], outs=[mix_all[g].ap().opt()]), ccsems[g], reads=deps,
                                     writes=(f"ccout{g}",))

                        def tail1b(tail1=tail1, part_b=part_b):
                            tail1()
                            pend["b"] = part_b

                        assert not ftail
                        ftail.extend([tail0, tail1b])
                        if h == 0 and Q == NT - 1:
                            while pstate["n"] <= NPRE:
                                pre_step()
                            evs = [(P.esem["pe"], P.esem["pe"].n)] + [(s_, s_.n) for s_ in posem]
                            P._emit_waits("sp", evs)
                            P.dma("sp", Wo[:, :, :], WoS.ap()[:, :, :], wpsem[0], writes=("Wo",))
                            for c_ in range(2):
                                P.dma("sp", WgL[c_][:, :, :], WgS[c_].ap()[:, :, :], wpsem[1 + 2 * c_], writes=(f"Wg{c_}",))
                                P.dma("sp", WuL[c_][:, :, :], WuS[c_].ap()[:, :, :], wpsem[2 + 2 * c_], writes=(f"Wu{c_}",))
                while ftail:
                    ftail.pop(0)()
                if pend["b"] is not None:
                    pend["b"]()
                    pend["b"] = None
                while pstate["n"] <= len(pjobs):
                    pre_step()
                P.barrier(exclude=ccsems, keep_prefix="cc")
                P.emit()

        if debug:
            dsem = P.new_sem("dsem")
            for g in range(NG):
                P.dma("sp", dbg_mix[:, g * CH:(g + 1) * CH], mix_in[g].ap()[:, :], dsem)
            P.barrier()
            P.emit()

        with ExitStack() as wst:
            for c_ in (2, 3):
                WgL[c_] = sb(wst, f"Wg{c_}", [128, 8, CW[c_]], BF16)
                WuL[c_] = sb(wst, f"Wu{c_}", [128, 8, CW[c_]], BF16)
            Wd = sb(wst, "Wd", [128, NM, D], BF16)
            with ExitStack() as st:
                TW, NS = 256, 2
                NT4 = TOKC // TW
                mc = [sb(st, f"mc{c}", [128, 2, 128], BF16) for c in range(4)]
                mcsem = [P.new_sem(f"mcsem{c}") for c in range(4)]
                sel = [WA[:, 31744:32768].rearrange("p (k c) -> p k c", k=8), sb(st, "sel1", [128, 8, 128], BF16)]
                xq = [sb(st, f"xq{i}", [128, D], F32) for i in range(NS)]
                xqsem = [P.new_sem(f"xqsem{i}") for i in range(NS)]
                xo = sb(st, "xo", [128, D], F32)
                xosem = P.new_sem("xosem")
                x1 = sb(st, "x1", [128, NS, D], F32)
                xn2 = WA[:, 30720:31744]
                h2T = sb(st, "h2T", [128, 8, TW], BF16)
                sgl = sb(st, "sgl", [128, 2, TW], F32)
                aT = sb(st, "aT", [128, NM, TW], BF16)
                gf = sb(st, "gf", [128, D], F32)
                stat = sb(st, "stat4", [128, 8], F32)
                po = [ps(st, f"po{i}", [128, 512]) for i in range(2)]
                ptr = ps(st, "ptr4", [128, 8, 128], BF16)
                pg = [ps(st, f"pg{i}", [128, 512]) for i in range(2)]
                pu = [ps(st, f"pu{i}", [128, 512]) for i in range(2)]
                gsem = P.new_sem("gsem")
                P.dma("sp", gf[:, :], gfb[:, :], gsem, writes=("gf",))
                wsems = [P.new_sem(f"wsem{i}") for i in range(11)]

                def bulk_loads():
                    for c in (2, 3):
                        P.dma("act", WgL[c][:, :, :], WgS[c].ap()[:, :, :], wsems[1 + 2 * c], reads=("sel1_3",), writes=(f"Wg{c}",))
                        P.dma("act", WuL[c][:, :, :], WuS[c].ap()[:, :, :], wsems[2 + 2 * c], writes=(f"Wu{c}",))
                    P.dma("act", Wd[:, 0:11, :], WdS.ap()[:, 0:11, :], wsems[9], writes=("Wd0",))
                    P.dma("act", Wd[:, 11:22, :], WdS.ap()[:, 11:22, :], wsems[10], writes=("Wd1",))
                mall = [m_.ap().rearrange("(fc p) t -> p fc t", p=128) for m_ in mix_all]

                def pre(t):
                    rounds = []
                    for sub in range(NS):
                        T0 = t * TW + sub * 128
                        P.dma("sp", xq[sub][:, :], xres[T0:T0 + 128, :], xqsem[sub], writes=(f"xq{sub}",))
                        for g4 in range(4):
                            def rnd(sub=sub, g4=g4, T0=T0):
                                fs = slice(2 * g4, 2 * g4 + 2)
                                for c in range(4):
                                    tk = c * TOKC + T0
                                    P.dma("sp", mc[c][:, :, :], mall[tk // CH][:, fs, tk % CH:tk % CH + 128], mcsem[c],
                                          reads=(f"ccout{tk // CH}",), writes=(f"mc{c}",))
                                P.op("dve", lambda e: e.tensor_scalar(out=sel[sub][:, fs, :], in0=mc[0][:, :, :], scalar1=oh[:, 0:1],
                                                                      scalar2=None, op0=ALU.mult),
                                     reads=("mc0",), writes=(f"sel{sub}_{g4}",))
                                for c in range(1, 4):
                                    P.op("dve", lambda e, c=c: e.scalar_tensor_tensor(out=sel[sub][:, fs, :], in0=mc[c][:, :, :], scalar=oh[:, c:c + 1],
                                                                                      in1=sel[sub][:, fs, :], op0=ALU.mult, op1=ALU.add),
                                         reads=(f"mc{c}", f"sel{sub}_{g4}"), writes=(f"sel{sub}_{g4}",))
                            rounds.append(rnd)
                    return rounds

                def outproj(t, sub, banks, bkeys):
                    skeys = tuple(f"sel{sub}_{g4}" for g4 in range(4))
                    for hf in range(2):
                        fns = [(lambda e, fc=fc, hf=hf: e.matmul(banks[hf][:, :], lhsT=sel[sub][:, fc, :], rhs=Wo[:, fc, hf * 512:(hf + 1) * 512],
                                                                 start=(fc == 0), stop=(fc == 7))) for fc in range(8)]
                        P.group("pe", fns, reads=skeys + ("Wo",), writes=(bkeys[hf],))
                        P.op("dve", lambda e, hf=hf: e.tensor_tensor(out=x1[:, sub, hf * 512:(hf + 1) * 512], in0=banks[hf][:, :],
                                                                    in1=xq[sub][:, hf * 512:(hf + 1) * 512], op=ALU.add),
                             reads=(bkeys[hf], f"xq{sub}"), writes=(f"x1_{sub}_{hf}",))

                def chain(t, sub):
                    xk = (f"x1_{sub}_0", f"x1_{sub}_1")
                    P.op("act", lambda e: e.activation(out=xn2[:, :], in_=x1[:, sub, :], func=AF.Square, accum_out=stat[:, 0:1]),
                         reads=xk, writes=("xn2", "ss0"))
                    P.op("act", lambda e: e.activation(out=stat[:, 1:2], in_=stat[:, 0:1], func=AF.Ln, scale=1.0 / D, bias=EPS), reads=("ss0",), writes=("ln0",))
                    P.op("act", lambda e: e.activation(out=stat[:, 2:3], in_=stat[:, 1:2], func=AF.Exp, scale=-0.5), reads=("ln0",), writes=("rstd0",))
                    P.op("dve", lambda e: e.tensor_scalar(out=xn2[:, :], in0=x1[:, sub, :], scalar1=stat[:, 2:3], scalar2=None, op0=ALU.mult),
                         reads=xk + ("rstd0",), writes=("xn2",))
                    fns = [(lambda e, kc=kc: e.transpose(out=ptr[:, kc, :], in_=xn2[:, kc * 128:(kc + 1) * 128], identity=ident)) for kc in range(8)]
                    P.group("pe", fns, reads=("xn2",), writes=("ptr4",))
                    P.op("act", lambda e: e.activation(out=h2T[:, :, sub * 128:(sub + 1) * 128], in_=ptr[:, :, :], func=AF.Copy),
                         reads=("ptr4",), writes=(f"h2T{sub}",))

                def mloop(t, hooks=()):
                    hkeys = tuple(f"h2T{s_}" for s_ in range(NS))
                    hooks = list(hooks)
                    for m in range(NM):
                        if m >= 2 and m % 2 == 0 and hooks:
                            hooks.pop(0)()
                        b_ = m % 2
                        wc_ = max(c_ for c_ in range(4) if CS[c_] <= m * 128)
                        lc = m * 128 - CS[wc_]
                        for (W_, pp, key, wn) in ((WgL[wc_], pg[b_], f"pg{b_}", "Wg"), (WuL[wc_], pu[b_], f"pu{b_}", "Wu")):
                            fns = [(lambda e, kc=kc, W_=W_, pp=pp, lc=lc: e.matmul(pp[:, 0:TW], lhsT=W_[:, kc, lc:lc + 128], rhs=h2T[:, kc, :],
                                                                                   start=(kc == 0), stop=(kc == 7))) for kc in range(8)]
                            P.group("pe", fns, reads=hkeys + (f"{wn}{wc_}",), writes=(key,))
                        P.op("act", lambda e, b_=b_: e.activation(out=sgl[:, b_, :], in_=pg[b_][:, 0:TW], func=AF.Silu), reads=(f"pg{b_}",), writes=(f"sgl{b_}",))
                        P.op("dve", lambda e, b_=b_, m=m: e.tensor_tensor(out=aT[:, m, :], in0=pu[b_][:, 0:TW], in1=sgl[:, b_, :], op=ALU.mult),
                             reads=(f"pu{b_}", f"sgl{b_}"), writes=(f"aT{m}",))

                def down(t, sub):
                    akeys = tuple(f"aT{m}" for m in range(NM))
                    T0 = t * TW + sub * 128
                    for hf in range(2):
                        fns = [(lambda e, m=m, hf=hf: e.matmul(po[hf][:, :], lhsT=aT[:, m, sub * 128:(sub + 1) * 128],
                                                               rhs=Wd[:, m, hf * 512:(hf + 1) * 512], start=(m == 0), stop=(m == NM - 1)))
                               for m in range(NM)]
                        P.group("pe", fns, reads=akeys + ("Wd0", "Wd1"), writes=(f"po{hf}",))
                        P.op("dve", lambda e, hf=hf: e.tensor_tensor(out=xo[:, hf * 512:(hf + 1) * 512], in0=po[hf][:, :],
                                                                    in1=x1[:, sub, hf * 512:(hf + 1) * 512], op=ALU.add),
                             reads=(f"po{hf}", f"x1_{sub}_{hf}"), writes=(f"xo_{hf}",))
                    P.op("act", lambda e: e.activation(out=xn2[:, :], in_=xo[:, :], func=AF.Square, accum_out=stat[:, 4:5]),
                         reads=("xo_0", "xo_1"), writes=("xn2", "ss4"))
                    P.op("act", lambda e: e.activation(out=stat[:, 5:6], in_=stat[:, 4:5], func=AF.Ln, scale=1.0 / D, bias=EPS), reads=("ss4",), writes=("ln4",))
                    P.op("act", lambda e: e.activation(out=stat[:, 6:7], in_=stat[:, 5:6], func=AF.Exp, scale=-0.5), reads=("ln4",), writes=("rstd4",))
                    P.op("dve", lambda e: e.scalar_tensor_tensor(out=xo[:, :], in0=xo[:, :], scalar=stat[:, 6:7], in1=gf[:, :], op0=ALU.mult, op1=ALU.mult),
                         reads=("xo_0", "xo_1", "rstd4", "gf"), writes=("xo_0", "xo_1"))
                    P.dma("sp", out[T0:T0 + 128, :], xo[:, :], xosem, reads=("xo_0", "xo_1"))

                r0 = pre(0)
                for r_ in r0:
                    r_()
                bulk_loads()
                outproj(0, 0, po, ("po0", "po1"))
                outproj(0, 1, pg, ("pg0", "pg1"))
                chain(0, 0)
                chain(0, 1)
                for t in range(NT4):
                    nxt = t + 1 < NT4
                    mloop(t, pre(t + 1) if nxt else ())
                    down(t, 0)
                    if nxt:
                        outproj(t + 1, 0, pg, ("pg0", "pg1"))
                    down(t, 1)
                    if nxt:
                        chain(t + 1, 0)
                        outproj(t + 1, 1, pu, ("pu0", "pu1"))
                        chain(t + 1, 1)
                P.barrier()
                P.emit()
    return nc


def _consts(S):
    f32 = np.float32
    inv_freq = (500000.0 ** (-(np.arange(0, 16, 2, dtype=f32) / f32(16)))).astype(f32)
    ang = (np.arange(S, dtype=f32)[:, None] * inv_freq[None, :]).astype(f32)
    cos, sin = np.cos(ang).astype(f32), np.sin(ang).astype(f32)
    C = np.ones((128, S), f32)
    Sg = np.zeros((128, S), f32)
    for p in range(128):
        d = p % 64
        if d < 8:
            C[p] = cos[:, d]
            Sg[p] = -sin[:, d]
        elif d < 16:
            C[p] = cos[:, d - 8]
            Sg[p] = sin[:, d - 8]
    k = np.arange(128)[:, None]
    q = np.arange(128)[None, :]
    ident = (k == q).astype(f32)
    maskc = np.where(k <= q, 0.0, NEGM).astype(f32)
    maskd = np.where((k >= 64) & (q < 64), NEGM, 0.0).astype(f32)
    cbf = np.concatenate([ident, maskc, maskd], axis=1).astype(ml_dtypes.bfloat16)
    pc = np.zeros((128, 6), f32)
    pc[64:70, 0] = (0, 1, 1, 0, 1, 1)
    pc[64:70, 1] = (0, 0, 1, 0, 0, 1)
    pc[64:70, 2] = (1, 1, 1, 0, 0, 0)
    pc[64:70, 3] = (0, 0, 0, 1, 1, 1)
    pc[64:70, 4] = (0, 0, 0, -1, -1, -1)
    pc[64:70, 5] = (1, 1, 1, 0, 0, 0)
    swf = (k == (q + 64) % 128).astype(f32)
    return C, Sg, cbf, pc, swf


def _group_cols(j):
    def swap(base):
        cols = []
        for p in range(128):
            c_, d = divmod(p, 64)
            pd = d + 8 if d < 8 else (d - 8 if d < 16 else d)
            cols.append(base + j * 128 + c_ * 64 + pd)
        return cols

    cols = list(range(j * 128, (j + 1) * 128)) + swap(0)
    cols += list(range(512 + j * 128, 512 + (j + 1) * 128)) + swap(512)
    h0, h1 = 2 * j, 2 * j + 1
    cols += list(range(1536 + h0 * 64, 1536 + h0 * 64 + 64)) + list(range(1536 + h1 * 64, 1536 + h1 * 64 + 64))
    cols += list(range(2048 + h0 * 64, 2048 + h0 * 64 + 64)) + [3072 + h0] * 6
    cols += list(range(2048 + h1 * 64, 2048 + h1 * 64 + 64)) + [3072 + h1] * 6
    cols += list(range(1024 + j * 128, 1024 + (j + 1) * 128))
    cols += list(range(2560 + h0 * 64, 2560 + h0 * 64 + 64)) + list(range(2560 + h1 * 64, 2560 + h1 * 64 + 64))
    assert len(cols) == NCOL
    return np.array(cols)


def make_in_maps(x, norm1_g, w_in, b_f, lam_q1, lam_k1, lam_q2, lam_k2, subln_g, w_out, norm2_g, w_gate, w_up, w_down, normf_g):
    f32 = np.float32
    a = lambda t: np.ascontiguousarray(np.asarray(t, dtype=f32))
    x = a(x)
    B, S, _ = x.shape
    TOKC = S // 4
    C, Sg, cbf, pc, swf = _consts(S)
    w_in0, w_out0 = a(w_in)[0], a(w_out)[0]
    perm = np.concatenate([np.concatenate([np.arange(r * 128, (r + 1) * 128), np.arange(512 + r * 128, 512 + (r + 1) * 128)]) for r in range(4)])
    w_out_p = np.ascontiguousarray(w_out0[perm])
    g1T = np.ascontiguousarray(a(norm1_g)[0].reshape(8, 128).T)
    g2T = np.ascontiguousarray(a(norm2_g)[0].reshape(8, 128).T)
    lamv = np.ascontiguousarray(np.broadcast_to(np.concatenate([a(lam_q1)[0], a(lam_k1)[0], a(lam_q2)[0], a(lam_k2)[0]])[None, :], (128, 256)))
    sgv = np.ascontiguousarray(a(subln_g)[0].reshape(128, 1))
    gfb = np.ascontiguousarray(np.broadcast_to(a(normf_g)[None, :], (128, D)))
    wg, wu, wd = a(w_gate)[0], a(w_up)[0], a(w_down)[0]
    bf0 = a(b_f)[0]
    maps = []
    for c in range(8):
        b, j = divmod(c, 4)
        bfp = np.zeros((128, 2), f32)
        bfp[64:70, 0] = bf0[2 * j]
        bfp[64:70, 1] = bf0[2 * j + 1]
        oh = np.zeros((128, 4), f32)
        oh[:, j] = 1.0
        maps.append({
            "xb": x[b], "xres": np.ascontiguousarray(x[b, j * TOKC:(j + 1) * TOKC]),
            "w_in": np.ascontiguousarray(w_in0[:, _group_cols(j)]),
            "g1T": g1T, "g2T": g2T, "bfp": bfp, "lamv": lamv, "sgv": sgv, "pcv": pc, "ohv": oh, "gfb": gfb,
            "ropeC": C, "ropeS": Sg, "cbf": cbf, "swfd": swf, "w_out": w_out_p, "w_gate": wg, "w_up": wu, "w_down": wd,
        })
    return maps, B, S


_NC_CACHE = {}


def kernel(**inputs):
    maps, B, S = make_in_maps(**inputs)
    if S not in _NC_CACHE:
        _NC_CACHE[S] = build(S)
    res = run_bass_kernel_spmd(_NC_CACHE[S], maps, core_ids=list(range(8)))
    TOKC = S // 4
    outp = np.empty((B, S, D), np.float32)
    for c in range(8):
        b, j = divmod(c, 4)
        outp[b, j * TOKC:(j + 1) * TOKC] = res.results[c]["out"]
    return outp
```

```python
from contextlib import ExitStack

import ml_dtypes
import numpy as np

import concourse.bass as bass
import concourse.mybir as mybir
from concourse.bass_utils import run_bass_kernel_spmd

F32 = mybir.dt.float32
BF16 = mybir.dt.bfloat16
AF = mybir.ActivationFunctionType
ALU = mybir.AluOpType
AX = mybir.AxisListType

D = 1024
DFF = 2816
NM = DFF // 128
EPS = 1e-5
LAM_INIT = 0.8 - 0.6 * 1.0
NEGM = -30000.0
ENGS = ("pe", "act", "dve", "pool", "sp")
C_DQ, C_DQS, C_DK, C_DKS, C_FQ0, C_FQ1, C_FK0, C_FK1, C_V, NCOL = 0, 128, 256, 384, 512, 576, 640, 710, 780, 1036


class Sem:
    def __init__(self, h, eng):
        self.h = h
        self.eng = eng
        self.n = 0


class Prog:
    def __init__(self, nc, stack):
        self.nc = nc
        self.stack = stack
        self.esem = {e: Sem(stack.enter_context(nc.semaphore("s_" + e)), e) for e in ("pe", "act", "dve", "pool")}
        self.all_sems = list(self.esem.values())
        self.waited = {e: {} for e in ENGS}
        self.q = {e: [] for e in ENGS}
        self.lastw = {}
        self.readers = {}

    def new_sem(self, name):
        s = Sem(self.stack.enter_context(self.nc.semaphore(name)), None)
        self.all_sems.append(s)
        return s

    def _emit_waits(self, eng, evs):
        w = self.waited[eng]
        for s, v in evs:
            if w.get(id(s), 0) < v:
                w[id(s)] = v
                self.q[eng].append(lambda e, h=s.h, v=v: e.wait_ge(h, v))

    def _deps(self, eng, reads, writes):
        evs = []
        for k in reads:
            ev = self.lastw.get(k)
            if ev is not None and not (eng == "pe" and ev[0].eng == "pe"):
                evs.append(ev)
        for k in writes:
            ev = self.lastw.get(k)
            if ev is not None and not (eng == "pe" and ev[0].eng == "pe"):
                evs.append(ev)
            for ev in self.readers.get(k, ()):
                if not (eng == "pe" and ev[0].eng == "pe"):
                    evs.append(ev)
        return evs

    def _record(self, ev, reads, writes):
        for k in writes:
            self.lastw[k] = ev
            self.readers[k] = []
        for k in reads:
            self.readers.setdefault(k, []).append(ev)

    def op(self, eng, fn, reads=(), writes=(), signal=True):
        self._emit_waits(eng, self._deps(eng, reads, writes))
        if signal:
            s = self.esem[eng]
            s.n += 1
            self.q[eng].append(lambda e, fn=fn, h=s.h: fn(e).then_inc(h, 1))
            self._record((s, s.n), reads, writes)
        else:
            self.q[eng].append(lambda e, fn=fn: fn(e))

    def group(self, eng, fns, reads=(), writes=()):
        self._emit_waits(eng, self._deps(eng, reads, writes))
        for fn in fns[:-1]:
            self.q[eng].append(lambda e, fn=fn: fn(e))
        s = self.esem[eng]
        s.n += 1
        self.q[eng].append(lambda e, fn=fns[-1], h=s.h: fn(e).then_inc(h, 1))
        self._record((s, s.n), reads, writes)

    def dma(self, eng, out, in_, sem, reads=(), writes=(), **kw):
        self._emit_waits(eng, self._deps(eng, reads, writes))
        sem.n += 16
        self.q[eng].append(lambda e, o=out, i=in_, h=sem.h, kw=kw: e.dma_start(out=o, in_=i, **kw).then_inc(h, 16))
        self._record((sem, sem.n), reads, writes)

    def cc(self, fn, sem, reads=(), writes=()):
        self._emit_waits("pool", self._deps("pool", reads, ("cc_chain",)))
        sem.n += 1
        self.q["pool"].append(lambda e, fn=fn, h=sem.h: fn(e).then_inc(h, 1))
        self._record((sem, sem.n), reads, ("cc_chain",) + tuple(writes))

    def raw(self, eng, fn):
        self.q[eng].append(fn)

    def barrier(self, exclude=(), keep_prefix=None):
        ex = {id(s) for s in exclude}
        evs = [(s, s.n) for s in self.all_sems if s.n > 0 and id(s) not in ex]
        for e in ENGS:
            self._emit_waits(e, evs)
        keep = {k: v for k, v in self.lastw.items() if keep_prefix and k.startswith(keep_prefix)}
        self.lastw.clear()
        self.readers.clear()
        self.lastw.update(keep)

    def emit(self):
        q = self.q
        with self.nc.Block() as blk:
            @blk.tensor
            def _(e):
                for f in q["pe"]:
                    f(e)

            @blk.scalar
            def _(e):
                for f in q["act"]:
                    f(e)

            @blk.vector
            def _(e):
                for f in q["dve"]:
                    f(e)

            @blk.gpsimd
            def _(e):
                for f in q["pool"]:
                    f(e)

            @blk.sync
            def _(e):
                for f in q["sp"]:
                    f(e)
        self.q = {e: [] for e in ENGS}


def build(S, debug=False):
    NT = S // 512
    TOKC = S // 4
    NSUB4 = TOKC // 128
    nc = bass.Bass("TRN2", target_bir_lowering=False)

    def din(name, shape, dt=F32):
        return nc.dram_tensor(name, shape, dt, kind="ExternalInput").ap()

    xb = din("xb", [S, D])
    xres = din("xres", [TOKC, D])
    w_in = din("w_in", [D, NCOL])
    g1T = din("g1T", [128, 8])
    g2T = din("g2T", [128, 8])
    bfp = din("bfp", [128, 2])
    lamv = din("lamv", [128, 256])
    sgv = din("sgv", [128, 1])
    pcv = din("pcv", [128, 6])
    ohv = din("ohv", [128, 4])
    gfb = din("gfb", [128, D])
    ropeC = din("ropeC", [128, S])
    ropeS = din("ropeS", [128, S])
    cbf = din("cbf", [128, 384], BF16)
    swfd = din("swfd", [128, 128])
    w_out = din("w_out", [D, D])
    w_gate = din("w_gate", [D, DFF])
    w_up = din("w_up", [D, DFF])
    w_down = din("w_down", [DFF, D])
    out = nc.dram_tensor("out", [TOKC, D], F32, kind="ExternalOutput").ap()
    CH = min(1024, S)
    NG = S // CH
    mix_in = [nc.dram_tensor(f"mix_in{g}", [256, CH], BF16) for g in range(NG)]
    mix_all = [nc.dram_tensor(f"mix_all{g}", [1024, CH], BF16) for g in range(NG)]
    CW = [768, 640, 768, 640]
    CS = [0, 768, 1408, 2176]
    WoS = nc.dram_tensor("WoS", [128, 8, D], BF16)
    WgS = [nc.dram_tensor(f"WgS{c}", [128, 8, CW[c]], BF16) for c in range(4)]
    WuS = [nc.dram_tensor(f"WuS{c}", [128, 8, CW[c]], BF16) for c in range(4)]
    WdS = nc.dram_tensor("WdS", [128, NM, D], BF16)
    dbg_mix = nc.dram_tensor("dbg_mix", [256, S], BF16, kind="ExternalOutput").ap() if debug else None

    with ExitStack() as gst:
        P = Prog(nc, gst)
        sb = lambda st, name, shape, dt: st.enter_context(nc.sbuf_tensor(name, shape, dt))
        ps = lambda st, name, shape, dt=F32: st.enter_context(nc.psum_tensor(name, shape, dt))

        cb = sb(gst, "cb", [128, 384], BF16)
        ident, maskc, maskd = cb[:, 0:128], cb[:, 128:256], cb[:, 256:384]
        onesb = sb(gst, "onesb", [128, 128], BF16)
        onesf = sb(gst, "onesf", [128, 128], F32)
        swf = sb(gst, "swf", [128, 128], F32)
        g1s = sb(gst, "g1s", [128, 8], F32)
        g2s = sb(gst, "g2s", [128, 8], F32)
        negb = sb(gst, "negb", [128, 2], F32)
        lams = sb(gst, "lams", [128, 256], F32)
        lamp = sb(gst, "lamp", [128, 128], F32)
        lsc = sb(gst, "lsc", [128, 8], F32)
        pc = sb(gst, "pc", [128, 6], F32)
        oh = sb(gst, "oh", [128, 4], F32)
        neglam = lsc[:, 4:5]
        sg = lsc[:, 5:6]
        csem = P.new_sem("csem")
        ccsems = [P.new_sem(f"ccsem{g}") for g in range(NG)]

        WA = sb(gst, "WA", [128, 32768], BF16)
        QdT = WA[:, 0:S]
        KdT = WA[:, 8192:8192 + S]
        Wo = WA[:, 0:8192].rearrange("p (k c) -> p k c", k=8)
        WgL, WuL = [None] * 4, [None] * 4
        WgL[0] = WA[:, 8192:14336].rearrange("p (k c) -> p k c", k=8)
        WuL[0] = WA[:, 14336:20480].rearrange("p (k c) -> p k c", k=8)
        WgL[1] = WA[:, 20480:25600].rearrange("p (k c) -> p k c", k=8)
        WuL[1] = WA[:, 25600:30720].rearrange("p (k c) -> p k c", k=8)
        with ExitStack() as ast:
            QfT = [WA[:, 16384:16384 + S], sb(ast, "QfT1", [128, S], BF16)]
            KfT = [WA[:, 24576:24576 + S], sb(ast, "KfT1", [128, S], BF16)]
            Vall = sb(ast, "Vall", [128, S // 128, 256], BF16)

            with ExitStack() as p1st:
                w_bf = sb(p1st, "w_bf", [128, 8, NCOL], BF16)
                with ExitStack() as st:
                    NSTG = 4
                    stg = [sb(st, f"stg{i}", [128, NCOL], F32) for i in range(NSTG)]
                    ssem = [P.new_sem(f"stg_s{i}") for i in range(NSTG)]
                    for (dst, src) in ((cb, cbf), (g1s, g1T), (g2s, g2T), (negb, bfp), (lams, lamv), (pc, pcv), (oh, ohv), (swf, swfd)):
                        P.dma("sp", dst[:, :], src[:, :], csem, writes=("consts",))
                    P.dma("sp", lsc[:, 5:6], sgv[:, :], csem, writes=("consts",))
                    P.op("dve", lambda e: e.memset(onesb[:, :], 1.0), writes=("onesb",))
                    P.op("dve", lambda e: e.memset(onesf[:, :], 1.0), writes=("onesf",))
                    P.op("dve", lambda e: e.tensor_scalar(out=negb[64:70, :], in0=negb[64:70, :], scalar1=-1.0, scalar2=None, op0=ALU.mult),
                         reads=("consts",), writes=("negb",))
                    P.op("dve", lambda e: e.tensor_scalar(out=lsc[:, 5:6], in0=lsc[:, 5:6], scalar1=1.0 - LAM_INIT, scalar2=None, op0=ALU.mult),
                         reads=("consts",), writes=("sg",))
                    P.op("dve", lambda e: e.tensor_tensor(out=lamp[:, 0:64], in0=lams[:, 0:64], in1=lams[:, 64:128], op=ALU.mult),
                         reads=("consts",), writes=("lamp0",))
                    P.op("dve", lambda e: e.tensor_tensor(out=lamp[:, 64:128], in0=lams[:, 128:192], in1=lams[:, 192:256], op=ALU.mult),
                         reads=("consts",), writes=("lamp1",))
                    P.op("dve", lambda e: e.reduce_sum(out=lsc[:, 0:1], in_=lamp[:, 0:64], axis=AX.X), reads=("lamp0",), writes=("ls0",))
                    P.op("dve", lambda e: e.reduce_sum(out=lsc[:, 1:2], in_=lamp[:, 64:128], axis=AX.X), reads=("lamp1",), writes=("ls1",))
                    P.op("act", lambda e: e.activation(out=lsc[:, 2:4], in_=lsc[:, 0:2], func=AF.Exp), reads=("ls0", "ls1"), writes=("le",))
                    P.op("dve", lambda e: e.tensor_tensor(out=lsc[:, 4:5], in0=lsc[:, 3:4], in1=lsc[:, 2:3], op=ALU.subtract),
                         reads=("le",), writes=("nl0",))
                    P.op("dve", lambda e: e.tensor_scalar(out=lsc[:, 4:5], in0=lsc[:, 4:5], scalar1=-LAM_INIT, scalar2=None, op0=ALU.add),
                         reads=("nl0",), writes=("neglam",))
                    for kc in range(8):
                        sl = kc % NSTG
                        P.dma("act" if kc % 2 == 0 else "sp", stg[sl][:, :], w_in[kc * 128:(kc + 1) * 128, :], ssem[sl], writes=(f"stg{sl}",))
                        eng = "dve"
                        P.op(eng, lambda e, kc=kc, sl=sl: e.tensor_scalar(out=w_bf[:, kc, :], in0=stg[sl][:, :], scalar1=g1s[:, kc:kc + 1],
                                                                           scalar2=None, op0=ALU.mult),
                             reads=(f"stg{sl}", "consts"), writes=("w_bf",))
                    P.barrier()
                    P.emit()

                with ExitStack() as st:
                    xsb = [sb(st, f"xsb{i}", [128, D], F32) for i in range(2)]
                    xsem = [P.new_sem(f"xsem{i}") for i in range(2)]
                    xn = [sb(st, f"xn{i}", [128, D], BF16) for i in range(2)]
                    hT = [sb(st, f"hT{i}", [128, 8, 512], BF16) for i in range(2)]
                    cs = [sb(st, f"cs{i}", [128, 2, 512], F32) for i in range(2)]
                    cssem = [P.new_sem(f"cssem{i}") for i in range(2)]
                    t1 = sb(st, "t1", [128, 512], F32)
                    t2 = sb(st, "t2", [128, 512], F32)
                    zeros = sb(st, "zeros", [128, 512], F32)
                    spt = sb(st, "spt", [128, 512], F32)
                    cum = [[sb(st, f"cum{h}_{i}", [128, 512], F32) for i in range(2)] for h in range(2)]
                    hib = sb(st, "hib", [128, 512], BF16)
                    r1n = sb(st, "r1n", [128, 512], F32)
                    mnb = sb(st, "mnb", [128, 512], BF16)
                    r2 = r1n
                    stat = sb(st, "stat", [128, 8], F32)
                    ptr = [ps(st, f"ptr{i}", [128, 8, 128], BF16) for i in range(2)]
                    pj = [ps(st, f"pj{i}", [128, 512]) for i in range(6)]
                    pjn = [0]

                    def nextpj():
                        i = pjn[0] % 6
                        pjn[0] += 1
                        return pj[i], f"pj{i}"

                    P.op("pool", lambda e: e.memset(zeros[:, :], 0.0), writes=("zeros",))

                    def stage_a(t):
                        sl = t % 2
                        sc = 3 * (t % 2)
                        P.dma("sp", xsb[sl][:, :], xb[t * 128:(t + 1) * 128, :], xsem[sl], writes=(f"xsb{sl}",))
                        P.op("act", lambda e: e.activation(out=xn[sl][:, :], in_=xsb[sl][:, :], func=AF.Square, accum_out=stat[:, sc:sc + 1]),
                             reads=(f"xsb{sl}",), writes=(f"xn{sl}", f"ss{sl}"))
                        P.op("act", lambda e: e.activation(out=stat[:, sc + 1:sc + 2], in_=stat[:, sc:sc + 1], func=AF.Ln, scale=1.0 / D, bias=EPS),
                             reads=(f"ss{sl}",), writes=(f"lnv{sl}",))
                        P.op("act", lambda e: e.activation(out=stat[:, sc + 2:sc + 3], in_=stat[:, sc + 1:sc + 2], func=AF.Exp, scale=-0.5),
                             reads=(f"lnv{sl}",), writes=(f"rstd{sl}",))
                        P.op("dve", lambda e: e.tensor_scalar(out=xn[sl][:, :], in0=xsb[sl][:, :], scalar1=stat[:, sc + 2:sc + 3],
                                                              scalar2=None, op0=ALU.mult),
                             reads=(f"xsb{sl}", f"rstd{sl}"), writes=(f"xn{sl}",))

                    def stage_b(t):
                        sl = t % 2
                        sub = t % 4
                        hs = (t // 4) % 2
                        pt_ = ptr[t % 2]
                        fns = [(lambda e, kc=kc: e.transpose(out=pt_[:, kc, :], in_=xn[sl][:, kc * 128:(kc + 1) * 128], identity=ident)) for kc in range(8)]
                        P.group("pe", fns, reads=(f"xn{sl}",), writes=(f"ptr{t % 2}",))
                        if sub % 2 == 0:
                            P.op("act", lambda e: e.activation(out=hT[hs][:, :, sub * 128:(sub + 1) * 128], in_=pt_[:, :, :], func=AF.Copy),
                                 reads=(f"ptr{t % 2}",), writes=(f"hT{hs}_{sub}",))
                        else:
                            P.op("dve", lambda e: e.tensor_copy(out=hT[hs][:, :, sub * 128:(sub + 1) * 128], in_=pt_[:, :, :]),
                                 reads=(f"ptr{t % 2}",), writes=(f"hT{hs}_{sub}",))

                    def stage_c(i):
                        c0 = i * 512
                        csl = i % 2
                        hs = i % 2
                        hcur = hT[hs]
                        hkeys = tuple(f"hT{hs}_{s_}" for s_ in range(4))

                        def proj_fm(col0, M):
                            pt, key = nextpj()
                            fns = [(lambda e, kc=kc: e.matmul(pt[0:M, :], lhsT=w_bf[:, kc, col0:col0 + M], rhs=hcur[:, kc, :],
                                                              start=(kc == 0), stop=(kc == 7))) for kc in range(8)]
                            P.group("pe", fns, reads=hkeys, writes=(key,))
                            return pt, key

                        def vpart(sub):
                            t = 4 * i + sub
                            pt, key = nextpj()
                            fns = [(lambda e, kc=kc: e.matmul(pt[:, 0:256], lhsT=hcur[:, kc, sub * 128:(sub + 1) * 128],
                                                              rhs=w_bf[:, kc, C_V:C_V + 256], start=(kc == 0), stop=(kc == 7))) for kc in range(8)]
                            P.group("pe", fns, reads=(f"hT{hs}_{sub}",), writes=(key,))
                            P.op("act", lambda e: e.activation(out=Vall[:, t, :], in_=pt[:, 0:256], func=AF.Copy), reads=(key,))

                        def rope(cq, cqs, dst):
                            pa, ka = proj_fm(cq, 128)
                            pb_, kb = proj_fm(cqs, 128)
                            P.op("dve", lambda e: e.tensor_tensor(out=t1[:, :], in0=pa[:, :], in1=cs[csl][:, 0, :], op=ALU.mult),
                                 reads=(ka, f"cs{csl}"), writes=("t1",))
                            P.op("dve", lambda e: e.tensor_tensor(out=t2[:, :], in0=pb_[:, :], in1=cs[csl][:, 1, :], op=ALU.mult),
                                 reads=(kb, f"cs{csl}"), writes=("t2",))
                            P.op("pool", lambda e: e.tensor_tensor(out=dst[:, c0:c0 + 512], in0=t1[:, :], in1=t2[:, :], op=ALU.add),
                                 reads=("t1", "t2"))

                        def foxq(h):
                            pa, ka = proj_fm(C_FQ0 + 64 * h, 128)
                            P.op("act", lambda e: e.activation(out=QfT[h][0:64, c0:c0 + 512], in_=pa[0:64, :], func=AF.Copy, scale=0.125), reads=(ka,))

                        def foxk(h):
                            pa, ka = proj_fm(C_FK0 + 70 * h, 128)
                            P.op("act", lambda e: e.activation(out=KfT[h][0:64, c0:c0 + 512], in_=pa[0:64, :], func=AF.Copy), reads=(ka,))
                            R = slice(64, 70)
                            P.op("act", lambda e: e.activation(out=spt[R, :], in_=pa[R, :], func=AF.Exp, scale=-1.0, bias=negb[R, h:h + 1]),
                                 reads=(ka, "negb"), writes=("spt",))
                            P.op("act", lambda e: e.activation(out=spt[R, :], in_=spt[R, :], func=AF.Ln, bias=1.0), reads=("spt",), writes=("spt",))
                            cur, prev = cum[h][i % 2], cum[h][(i + 1) % 2]
                            init = 0.0 if i == 0 else prev[R, 511:512]
                            ck = f"cum{h}_{i % 2}"
                            P.op("dve", lambda e: e.tensor_tensor_scan(out=cur[R, :], data0=spt[R, :], data1=zeros[R, :], initial=init,
                                                                       op0=ALU.add, op1=ALU.add),
                                 reads=("spt", "zeros", f"cum{h}_{(i + 1) % 2}"), writes=(ck,))
                            P.op("pool", lambda e: e.tensor_copy(out=hib[R, :], in_=cur[R, :]), reads=(ck,), writes=("hib",))
                            P.op("dve", lambda e: e.scalar_tensor_tensor(out=r1n[R, :], in0=hib[R, :], scalar=pc[R, 0:1], in1=cur[R, :],
                                                                         op0=ALU.mult, op1=ALU.subtract), reads=("hib", ck), writes=("r1n",))
                            P.op("pool", lambda e: e.tensor_copy(out=mnb[R, :], in_=r1n[R, :]), reads=("r1n",), writes=("mnb",))
                            P.op("dve", lambda e: e.scalar_tensor_tensor(out=r2[R, :], in0=mnb[R, :], scalar=pc[R, 1:2], in1=r1n[R, :],
                                                                         op0=ALU.mult, op1=ALU.subtract), reads=("mnb", "r1n"), writes=("r1n",))
                            P.op("dve", lambda e: e.tensor_scalar(out=KfT[h][R, c0:c0 + 512], in0=r2[R, :], scalar1=pc[R, 2:3], scalar2=pc[R, 3:4],
                                                                  op0=ALU.mult, op1=ALU.add), reads=("r1n",))
                            P.op("dve", lambda e: e.tensor_scalar(out=QfT[h][R, c0:c0 + 512], in0=r2[R, :], scalar1=pc[R, 4:5], scalar2=pc[R, 5:6],
                                                                  op0=ALU.mult, op1=ALU.add), reads=("r1n",))

                        def part0():
                            P.dma("pool", cs[csl][:, 0, :], ropeC[:, c0:c0 + 512], cssem[csl], writes=(f"cs{csl}",))
                            P.dma("pool", cs[csl][:, 1, :], ropeS[:, c0:c0 + 512], cssem[csl], writes=(f"cs{csl}",))
                            foxk(0)
                            vpart(0)
                            rope(C_DQ, C_DQS, QdT)

                        def part1():
                            foxk(1)
                            vpart(1)
                            rope(C_DK, C_DKS, KdT)

                        def part2():
                            vpart(2)
                            foxq(0)

                        def part3():
                            vpart(3)
                            foxq(1)

                        return [part0, part1, part2, part3]

                    stage_a(0)
                    stage_a(1)
                    stage_b(0)
                    stage_a(2)
                    stage_b(1)
                    stage_a(3)
                    stage_b(2)
                    stage_b(3)
                    for i in range(NT):
                        parts = stage_c(i)
                        for sub in range(4):
                            if i + 1 < NT:
                                stage_a(4 * (i + 1) + sub)
                            parts[sub]()
                            if i + 1 < NT and sub >= 1:
                                stage_b(4 * (i + 1) + sub - 1)
                        if i + 1 < NT:
                            stage_b(4 * (i + 1) + 3)
                    P.barrier()
                    P.emit()

            with ExitStack() as st:
                NB = 3
                pt1 = [sb(st, f"pt1_{i}", [128, 512], BF16) for i in range(NB)]
                pt2 = [sb(st, f"pt2_{i}", [128, 512], BF16) for i in range(NB)]
                ft = [sb(st, f"ft{i}", [128, 512], F32) for i in range(6)]
                mo = [sb(st, f"mo{i}", [128, 512], BF16) for i in range(2)]
                mosem = [P.new_sem(f"mosem{i}") for i in range(2)]
                bank = [ps(st, f"bk{i}", [128, 512]) for i in range(8)]
                VF = sb(st, "VF", [128, S // 128, 192], BF16)
                NKT = S // 128
                for j_ in range(4):
                    kj = slice(j_ * NKT // 4, (j_ + 1) * NKT // 4)
                    P.op("pool", lambda e, kj=kj: e.tensor_copy(out=VF[:, kj, 0:64], in_=Vall[:, kj, 128:192]), writes=("VF0",))
                    P.op("pool", lambda e, kj=kj: e.tensor_copy(out=VF[:, kj, 128:192], in_=Vall[:, kj, 192:256]), writes=("VF1",))
                P.op("pool", lambda e: e.memset(VF[:, :, 64:128], 1.0), writes=("VF2",))
                O1, O2, S1, S2 = bank[0], bank[1], bank[2], bank[3]
                s1b, s2b = [bank[4], bank[5]], [bank[6], bank[7]]
                pstg = [sb(st, f"pstg{i}", [128, D], F32) for i in range(2)]
                pcb = [sb(st, f"pcb{i}", [128, D], BF16) for i in range(2)]
                pisem = [P.new_sem(f"pisem{i}") for i in range(2)]
                posem = [P.new_sem(f"posem{i}") for i in range(2)]
                pjobs = []
                for fc in range(8):
                    pjobs.append((w_out[fc * 128:(fc + 1) * 128, :], WoS[:, fc, :], D, (sg if fc % 2 == 0 else None)))
                for c in range(4):
                    for kc in range(8):
                        pjobs.append((w_gate[kc * 128:(kc + 1) * 128, CS[c]:CS[c] + CW[c]], WgS[c][:, kc, :], CW[c], g2s[:, kc:kc + 1]))
                        pjobs.append((w_up[kc * 128:(kc + 1) * 128, CS[c]:CS[c] + CW[c]], WuS[c][:, kc, :], CW[c], g2s[:, kc:kc + 1]))
                NPRE = 8 + 32
                for m in range(NM):
                    pjobs.append((w_down[m * 128:(m + 1) * 128, :], WdS[:, m, :], D, None))
                pstate = {"n": 0, "iters": 0}
                wpsem = [P.new_sem(f"wpsem{i}") for i in range(5)]
                TOT_IT = 3 * sum(4 * Q_ + 4 for Q_ in range(NT))
                PER = max(1, (TOT_IT - 8) // (len(pjobs) + 2))

                def pre_in(n):
                    src, dst, wd, gain = pjobs[n]
                    sl = n % 2
                    P.dma("sp", pstg[sl][:, 0:wd], src, pisem[sl], writes=(f"pstg{sl}",))

                def pre_cast_out(n):
                    src, dst, wd, gain = pjobs[n]
                    sl = n % 2
                    eng = "pool"
                    qw = wd // 4
                    for j_ in range(4):
                        cj = slice(j_ * qw, (j_ + 1) * qw)
                        if gain is None:
                            P.op(eng, lambda e, cj=cj: e.tensor_copy(out=pcb[sl][:, cj], in_=pstg[sl][:, cj]), reads=(f"pstg{sl}",), writes=(f"pcb{sl}",))
                        else:
                            P.op(eng, lambda e, cj=cj: e.tensor_scalar(out=pcb[sl][:, cj], in0=pstg[sl][:, cj], scalar1=gain, scalar2=None, op0=ALU.mult),
                                 reads=(f"pstg{sl}",), writes=(f"pcb{sl}",))
                    P.dma("sp", dst, pcb[sl][:, 0:wd], posem[sl], reads=(f"pcb{sl}",))

                def pre_step():
                    n = pstate["n"]
                    if n <= len(pjobs):
                        if n < len(pjobs):
                            pre_in(n)
                        if n >= 1:
                            pre_cast_out(n - 1)
                        pstate["n"] = n + 1

                T_DIFF, T_FOX = 1.84, 0.64
                n_it = sum(4 * Q_ + 4 for Q_ in range(NT))
                PACE = 0.92 * (n_it * T_DIFF + 2 * n_it * T_FOX) / (len(pjobs) + 1)
                pstate["acc"] = 0.0

                def pre_tick(w=T_DIFF):
                    pstate["acc"] += w
                    if pstate["acc"] >= PACE:
                        pstate["acc"] -= PACE
                        pre_step()

                def tiles_of(Q):
                    return [(kt, max(0, kt - 4 * Q)) for kt in range(4 * Q + 4)]

                it = 0
                pend = {"b": None}
                for Q in range(NT):
                    q0 = Q * 512
                    tl = tiles_of(Q)
                    last = len(tl) - 1

                    def scores(idx, it, tl=tl, q0=q0):
                        kt, n0 = tl[idx]
                        sl = it % 2
                        diag = kt >= (q0 // 128)
                        for (sbk, lo, key) in ((s1b[sl], 0, f"s1_{sl}"), (s2b[sl], 64, f"s2_{sl}")):
                            fns = [lambda e, sbk=sbk, lo=lo, kt=kt, n0=n0: e.matmul(sbk[:, n0 * 128:512], lhsT=KdT[lo:lo + 64, kt * 128:(kt + 1) * 128],
                                                                                     rhs=QdT[lo:lo + 64, q0 + n0 * 128:q0 + 512], start=True, stop=not diag)]
                            if diag:
                                fns.append(lambda e, sbk=sbk, n0=n0: e.matmul(sbk[:, n0 * 128:(n0 + 1) * 128], lhsT=ident, rhs=maskd, start=False, stop=True))
                            P.group("pe", fns, writes=(key,))

                    def exps(idx, it, tl=tl):
                        kt, n0 = tl[idx]
                        sl, bl = it % 2, it % NB
                        P.op("act", lambda e: e.activation(out=pt1[bl][:, n0 * 128:512], in_=s1b[sl][:, n0 * 128:512], func=AF.Exp, scale=0.125),
                             reads=(f"s1_{sl}",), writes=(f"pt1_{bl}",))
                        P.op("act", lambda e: e.activation(out=pt2[bl][:, n0 * 128:512], in_=s2b[sl][:, n0 * 128:512], func=AF.Exp, scale=0.125),
                             reads=(f"s2_{sl}",), writes=(f"pt2_{bl}",))

                    def pv(idx, it, tl=tl, last=last):
                        kt, n0 = tl[idx]
                        bl = it % NB
                        st_, sp_ = (idx == 0), (idx == last)
                        cs_ = slice(n0 * 128, 512)
                        fns = [
                            lambda e: e.matmul(O1[:, cs_], lhsT=Vall[:, kt, 0:128], rhs=pt1[bl][:, cs_], start=st_, stop=sp_),
                            lambda e: e.matmul(S1[:, cs_], lhsT=onesb[:, :], rhs=pt1[bl][:, cs_], start=st_, stop=sp_),
                            lambda e: e.matmul(O2[:, cs_], lhsT=Vall[:, kt, 0:128], rhs=pt2[bl][:, cs_], start=st_, stop=sp_),
                            lambda e: e.matmul(S2[:, cs_], lhsT=onesb[:, :], rhs=pt2[bl][:, cs_], start=st_, stop=sp_),
                        ]
                        P.group("pe", fns, reads=(f"pt1_{bl}", f"pt2_{bl}"), writes=("O1", "O2", "S1", "S2"))

                    for idx in range(len(tl)):
                        scores(idx, it + idx)
                        exps(idx, it + idx)
                        if idx >= 1:
                            pv(idx - 1, it + idx - 1)
                        if idx == min(7, last) and pend["b"] is not None:
                            pend["b"]()
                            pend["b"] = None
                        pre_tick()
                    pv(last, it + last)
                    it += len(tl)
                    r1_, o1_, r2_, o2_, od_, sq_ = ft
                    P.op("dve", lambda e: e.tensor_copy(out=r1_[:, :], in_=S1[:, :]), reads=("S1",), writes=("r1",))
                    P.op("dve", lambda e: e.tensor_copy(out=o1_[:, :], in_=O1[:, :]), reads=("O1",), writes=("o1",))
                    P.op("dve", lambda e: e.tensor_copy(out=r2_[:, :], in_=S2[:, :]), reads=("S2",), writes=("r2",))
                    P.op("dve", lambda e: e.tensor_copy(out=o2_[:, :], in_=O2[:, :]), reads=("O2",), writes=("o2",))
                    P.op("dve", lambda e: e.reciprocal(out=r1_[:, :], in_=r1_[:, :]), reads=("r1",), writes=("r1",))
                    P.op("dve", lambda e: e.tensor_tensor(out=o1_[:, :], in0=o1_[:, :], in1=r1_[:, :], op=ALU.mult), reads=("o1", "r1"), writes=("o1",))
                    P.op("dve", lambda e: e.reciprocal(out=r2_[:, :], in_=r2_[:, :]), reads=("r2",), writes=("r2",))
                    P.op("dve", lambda e: e.tensor_tensor(out=o2_[:, :], in0=o2_[:, :], in1=r2_[:, :], op=ALU.mult), reads=("o2", "r2"), writes=("o2",))
                    P.op("dve", lambda e: e.scalar_tensor_tensor(out=od_[:, :], in0=o2_[:, :], scalar=neglam, in1=o1_[:, :], op0=ALU.mult, op1=ALU.add),
                         reads=("o1", "o2"), writes=("od",))
                    P.op("dve", lambda e: e.tensor_tensor(out=sq_[:, :], in0=od_[:, :], in1=od_[:, :], op=ALU.mult), reads=("od",), writes=("sq",))

                    def part_b(Q=Q, q0=q0):
                        ssq = s1b[0]
                        P.group("pe", [lambda e: e.matmul(ssq[:, :], lhsT=onesf[:, :], rhs=sq_[:, :], start=True, stop=True)], reads=("sq",), writes=("s1_0",))
                        P.op("act", lambda e: e.activation(out=r1_[:, :], in_=ssq[:, :], func=AF.Ln, scale=1.0 / 128, bias=EPS), reads=("s1_0",), writes=("r1",))
                        P.op("act", lambda e: e.activation(out=r2_[:, :], in_=r1_[:, :], func=AF.Exp, scale=-0.5), reads=("r1",), writes=("r2",))
                        ms = Q % 2
                        P.op("dve", lambda e: e.tensor_tensor(out=mo[ms][:, :], in0=od_[:, :], in1=r2_[:, :], op=ALU.mult),
                             reads=("od", "r2"), writes=(f"mo{ms}",))
                        P.dma("sp", mix_in[q0 // CH][0:128, q0 % CH:q0 % CH + 512], mo[ms][:, :], mosem[ms], reads=(f"mo{ms}",))

                    pend["b"] = part_b
                if pend["b"] is not None:
                    pend["b"]()
                    pend["b"] = None
                P.barrier()
                Of = [bank[0], bank[1]]
                SWo = [bank[2], bank[3]]
                sfb = [bank[4], bank[5], bank[6], bank[7]]
                for k_ in range(4):
                    P.op("pool", lambda e, k_=k_: e.memset(ft[k_][:, :], 0.0), writes=(f"Rt{k_}",))
                it = 0
                ftail = []
                for h in range(2):
                    vlo = 128 + 64 * h
                    vf0 = 64 * h
                    orows = slice(0, 64) if h == 0 else slice(64, 128)
                    srows = slice(64, 128) if h == 0 else slice(0, 64)
                    for Q in range(NT):
                        q0 = Q * 512
                        tl = tiles_of(Q)
                        last = len(tl) - 1
                        ab = (h * NT + Q) % 2

                        def scores(idx, it, tl=tl, q0=q0, h=h):
                            kt, n0 = tl[idx]
                            sl = it % 4
                            diag = kt >= (q0 // 128)
                            sbk = sfb[sl]
                            fns = [lambda e: e.matmul(sbk[:, n0 * 128:512], lhsT=KfT[h][0:70, kt * 128:(kt + 1) * 128],
                                                      rhs=QfT[h][0:70, q0 + n0 * 128:q0 + 512], start=True, stop=not diag)]
                            if diag:
                                fns.append(lambda e: e.matmul(sbk[:, n0 * 128:(n0 + 1) * 128], lhsT=ident, rhs=maskc, start=False, stop=True))
                            P.group("pe", fns, writes=(f"sf_{sl}",))

                        def exps(idx, it, tl=tl):
                            kt, n0 = tl[idx]
                            sl, bl = it % 4, it % (2 * NB)
                            ptile = (pt1 + pt2)[bl]
                            P.op("act", lambda e: e.activation(out=ptile[:, n0 * 128:512], in_=sfb[sl][:, n0 * 128:512], func=AF.Exp),
                                 reads=(f"sf_{sl}",), writes=(f"pf_{bl}",))

                        def pv(idx, it, tl=tl, last=last, ab=ab, vf0=vf0):
                            kt, n0 = tl[idx]
                            bl = it % (2 * NB)
                            ptile = (pt1 + pt2)[bl]
                            st_, sp_ = (idx == 0), (idx == last)
                            cs_ = slice(n0 * 128, 512)
                            fns = [lambda e: e.matmul(Of[ab][:, cs_], lhsT=VF[:, kt, vf0:vf0 + 128], rhs=ptile[:, cs_], start=st_, stop=sp_)]
                            P.group("pe", fns, reads=(f"pf_{bl}", "VF0", "VF1", "VF2"), writes=(f"Of{ab}",))

                        for idx in range(len(tl)):
                            scores(idx, it + idx)
                            exps(idx, it + idx)
                            if idx >= 2:
                                pv(idx - 2, it + idx - 2)
                            elif ftail:
                                ftail.pop(0)()
                            if idx == 3 and pend["b"] is not None:
                                pend["b"]()
                                pend["b"] = None
                            pre_tick(T_FOX)
                        Rt = ft[2 * h + ab]
                        rb = ft[4 + ab]
                        ms = Q % 2

                        def tail0(pv=pv, it0=it, last=last):
                            pv(last - 1, it0 + last - 1)

                        def tail1(pv=pv, it0=it, last=last, Rt=Rt, ab=ab, srows=srows, h=h):
                            pv(last, it0 + last)
                            P.op("dve", lambda e: e.reciprocal(out=Rt[srows, :], in_=Of[ab][srows, :]),
                                 reads=(f"Of{ab}",), writes=(f"Rt{2 * h + ab}",))

                        it += len(tl)

                        def part_b(Rt=Rt, rb=rb, ab=ab, ms=ms, orows=orows, h=h, Q=Q, q0=q0, vlo=vlo):
                            P.group("pe", [lambda e: e.matmul(SWo[ab][:, :], lhsT=swf[:, :], rhs=Rt[:, :], start=True, stop=True)],
                                    reads=(f"Rt{2 * h + ab}",), writes=(f"SWo{ab}",))
                            P.op("act", lambda e: e.activation(out=rb[orows, :], in_=SWo[ab][orows, :], func=AF.Copy),
                                 reads=(f"SWo{ab}",), writes=(f"rb{ab}",))
                            P.op("dve", lambda e: e.tensor_tensor(out=mo[ms][orows, :], in0=Of[ab][orows, :], in1=rb[orows, :], op=ALU.mult),
                                 reads=(f"Of{ab}", f"rb{ab}"), writes=(f"mo{ms}",))
                            P.dma("sp", mix_in[q0 // CH][vlo:vlo + 64, q0 % CH:q0 % CH + 512], mo[ms][orows, :], mosem[ms], reads=(f"mo{ms}",),
                                  writes=(f"mixin_{h}_{Q}",))
                            if h == 1 and (q0 + 512) % CH == 0:
                                g = q0 // CH
                                deps = tuple(f"mixin_{hh}_{QQ}" for hh in range(2) for QQ in range(g * CH // 512, (g + 1) * CH // 512))
                                P.cc(lambda e: e.collective_compute("AllGather", ALU.bypass, replica_groups=[[0, 1, 2, 3], [4, 5, 6, 7]],
                                                                    ins=[mix_in[g].ap().opt()], outs=[mix_all[g].ap().opt()]), ccsems[g], reads=deps,
                                     writes=(f"ccout{g}",))

                        def tail1b(tail1=tail1, part_b=part_b):
                            tail1()
                            pend["b"] = part_b

                        assert not ftail
                        ftail.extend([tail0, tail1b])
                        if h == 0 and Q == NT - 1:
                            while pstate["n"] <= NPRE:
                                pre_step()
                            evs = [(P.esem["pe"], P.esem["pe"].n)] + [(s_, s_.n) for s_ in posem]
                            P._emit_waits("sp", evs)
                            P.dma("sp", Wo[:, :, :], WoS.ap()[:, :, :], wpsem[0], writes=("Wo",))
                            for c_ in range(2):
                                P.dma("sp", WgL[c_][:, :, :], WgS[c_].ap()[:, :, :], wpsem[1 + 2 * c_], writes=(f"Wg{c_}",))
                                P.dma("sp", WuL[c_][:, :, :], WuS[c_].ap()[:, :, :], wpsem[2 + 2 * c_], writes=(f"Wu{c_}",))
                while ftail:
                    ftail.pop(0)()
                if pend["b"] is not None:
                    pend["b"]()
                    pend["b"] = None
                while pstate["n"] <= len(pjobs):
                    pre_step()
                P.barrier(exclude=ccsems, keep_prefix="cc")
                P.emit()

        if debug:
            dsem = P.new_sem("dsem")
            for g in range(NG):
                P.dma("sp", dbg_mix[:, g * CH:(g + 1) * CH], mix_in[g].ap()[:, :], dsem)
            P.barrier()
            P.emit()

        with ExitStack() as wst:
            for c_ in (2, 3):
                WgL[c_] = sb(wst, f"Wg{c_}", [128, 8, CW[c_]], BF16)
                WuL[c_] = sb(wst, f"Wu{c_}", [128, 8, CW[c_]], BF16)
            Wd = sb(wst, "Wd", [128, NM, D], BF16)
            with ExitStack() as st:
                TW, NS = 256, 2
                NT4 = TOKC // TW
                mc = [sb(st, f"mc{c}", [128, 2, 128], BF16) for c in range(4)]
                mcsem = [P.new_sem(f"mcsem{c}") for c in range(4)]
                sel = [WA[:, 31744:32768].rearrange("p (k c) -> p k c", k=8), sb(st, "sel1", [128, 8, 128], BF16)]
                xq = [sb(st, f"xq{i}", [128, D], F32) for i in range(NS)]
                xqsem = [P.new_sem(f"xqsem{i}") for i in range(NS)]
                xo = sb(st, "xo", [128, D], F32)
                xosem = P.new_sem("xosem")
                x1 = sb(st, "x1", [128, NS, D], F32)
                xn2 = WA[:, 30720:31744]
                h2T = sb(st, "h2T", [128, 8, TW], BF16)
                sgl = sb(st, "sgl", [128, 2, TW], F32)
                aT = sb(st, "aT", [128, NM, TW], BF16)
                gf = sb(st, "gf", [128, D], F32)
                stat = sb(st, "stat4", [128, 8], F32)
                po = [ps(st, f"po{i}", [128, 512]) for i in range(2)]
                ptr = ps(st, "ptr4", [128, 8, 128], BF16)
                pg = [ps(st, f"pg{i}", [128, 512]) for i in range(2)]
                pu = [ps(st, f"pu{i}", [128, 512]) for i in range(2)]
                gsem = P.new_sem("gsem")
                P.dma("sp", gf[:, :], gfb[:, :], gsem, writes=("gf",))
                wsems = [P.new_sem(f"wsem{i}") for i in range(11)]

                def bulk_loads():
                    for c in (2, 3):
                        P.dma("act", WgL[c][:, :, :], WgS[c].ap()[:, :, :], wsems[1 + 2 * c], reads=("sel1_3",), writes=(f"Wg{c}",))
                        P.dma("act", WuL[c][:, :, :], WuS[c].ap()[:, :, :], wsems[2 + 2 * c], writes=(f"Wu{c}",))
                    P.dma("act", Wd[:, 0:11, :], WdS.ap()[:, 0:11, :], wsems[9], writes=("Wd0",))
                    P.dma("act", Wd[:, 11:22, :], WdS.ap()[:, 11:22, :], wsems[10], writes=("Wd1",))
                mall = [m_.ap().rearrange("(fc p) t -> p fc t", p=128) for m_ in mix_all]

                def pre(t):
                    rounds = []
                    for sub in range(NS):
                        T0 = t * TW + sub * 128
                        P.dma("sp", xq[sub][:, :], xres[T0:T0 + 128, :], xqsem[sub], writes=(f"xq{sub}",))
                        for g4 in range(4):
                            def rnd(sub=sub, g4=g4, T0=T0):
                                fs = slice(2 * g4, 2 * g4 + 2)
                                for c in range(4):
                                    tk = c * TOKC + T0
                                    P.dma("sp", mc[c][:, :, :], mall[tk // CH][:, fs, tk % CH:tk % CH + 128], mcsem[c],
                                          reads=(f"ccout{tk // CH}",), writes=(f"mc{c}",))
                                P.op("dve", lambda e: e.tensor_scalar(out=sel[sub][:, fs, :], in0=mc[0][:, :, :], scalar1=oh[:, 0:1],
                                                                      scalar2=None, op0=ALU.mult),
                                     reads=("mc0",), writes=(f"sel{sub}_{g4}",))
                                for c in range(1, 4):
                                    P.op("dve", lambda e, c=c: e.scalar_tensor_tensor(out=sel[sub][:, fs, :], in0=mc[c][:, :, :], scalar=oh[:, c:c + 1],
                                                                                      in1=sel[sub][:, fs, :], op0=ALU.mult, op1=ALU.add),
                                         reads=(f"mc{c}", f"sel{sub}_{g4}"), writes=(f"sel{sub}_{g4}",))
                            rounds.append(rnd)
                    return rounds

                def outproj(t, sub, banks, bkeys):
                    skeys = tuple(f"sel{sub}_{g4}" for g4 in range(4))
                    for hf in range(2):
                        fns = [(lambda e, fc=fc, hf=hf: e.matmul(banks[hf][:, :], lhsT=sel[sub][:, fc, :], rhs=Wo[:, fc, hf * 512:(hf + 1) * 512],
                                                                 start=(fc == 0), stop=(fc == 7))) for fc in range(8)]
                        P.group("pe", fns, reads=skeys + ("Wo",), writes=(bkeys[hf],))
                        P.op("dve", lambda e, hf=hf: e.tensor_tensor(out=x1[:, sub, hf * 512:(hf + 1) * 512], in0=banks[hf][:, :],
                                                                    in1=xq[sub][:, hf * 512:(hf + 1) * 512], op=ALU.add),
                             reads=(bkeys[hf], f"xq{sub}"), writes=(f"x1_{sub}_{hf}",))

                def chain(t, sub):
                    xk = (f"x1_{sub}_0", f"x1_{sub}_1")
                    P.op("act", lambda e: e.activation(out=xn2[:, :], in_=x1[:, sub, :], func=AF.Square, accum_out=stat[:, 0:1]),
                         reads=xk, writes=("xn2", "ss0"))
                    P.op("act", lambda e: e.activation(out=stat[:, 1:2], in_=stat[:, 0:1], func=AF.Ln, scale=1.0 / D, bias=EPS), reads=("ss0",), writes=("ln0",))
                    P.op("act", lambda e: e.activation(out=stat[:, 2:3], in_=stat[:, 1:2], func=AF.Exp, scale=-0.5), reads=("ln0",), writes=("rstd0",))
                    P.op("dve", lambda e: e.tensor_scalar(out=xn2[:, :], in0=x1[:, sub, :], scalar1=stat[:, 2:3], scalar2=None, op0=ALU.mult),
                         reads=xk + ("rstd0",), writes=("xn2",))
                    fns = [(lambda e, kc=kc: e.transpose(out=ptr[:, kc, :], in_=xn2[:, kc * 128:(kc + 1) * 128], identity=ident)) for kc in range(8)]
                    P.group("pe", fns, reads=("xn2",), writes=("ptr4",))
                    P.op("act", lambda e: e.activation(out=h2T[:, :, sub * 128:(sub + 1) * 128], in_=ptr[:, :, :], func=AF.Copy),
                         reads=("ptr4",), writes=(f"h2T{sub}",))

                def mloop(t, hooks=()):
                    hkeys = tuple(f"h2T{s_}" for s_ in range(NS))
                    hooks = list(hooks)
                    for m in range(NM):
                        if m >= 2 and m % 2 == 0 and hooks:
                            hooks.pop(0)()
                        b_ = m % 2
                        wc_ = max(c_ for c_ in range(4) if CS[c_] <= m * 128)
                        lc = m * 128 - CS[wc_]
                        for (W_, pp, key, wn) in ((WgL[wc_], pg[b_], f"pg{b_}", "Wg"), (WuL[wc_], pu[b_], f"pu{b_}", "Wu")):
                            fns = [(lambda e, kc=kc, W_=W_, pp=pp, lc=lc: e.matmul(pp[:, 0:TW], lhsT=W_[:, kc, lc:lc + 128], rhs=h2T[:, kc, :],
                                                                                   start=(kc == 0), stop=(kc == 7))) for kc in range(8)]
                            P.group("pe", fns, reads=hkeys + (f"{wn}{wc_}",), writes=(key,))
                        P.op("act", lambda e, b_=b_: e.activation(out=sgl[:, b_, :], in_=pg[b_][:, 0:TW], func=AF.Silu), reads=(f"pg{b_}",), writes=(f"sgl{b_}",))
                        P.op("dve", lambda e, b_=b_, m=m: e.tensor_tensor(out=aT[:, m, :], in0=pu[b_][:, 0:TW], in1=sgl[:, b_, :], op=ALU.mult),
                             reads=(f"pu{b_}", f"sgl{b_}"), writes=(f"aT{m}",))

                def down(t, sub):
                    akeys = tuple(f"aT{m}" for m in range(NM))
                    T0 = t * TW + sub * 128
                    for hf in range(2):
                        fns = [(lambda e, m=m, hf=hf: e.matmul(po[hf][:, :], lhsT=aT[:, m, sub * 128:(sub + 1) * 128],
                                                               rhs=Wd[:, m, hf * 512:(hf + 1) * 512], start=(m == 0), stop=(m == NM - 1)))
                               for m in range(NM)]
                        P.group("pe", fns, reads=akeys + ("Wd0", "Wd1"), writes=(f"po{hf}",))
                        P.op("dve", lambda e, hf=hf: e.tensor_tensor(out=xo[:, hf * 512:(hf + 1) * 512], in0=po[hf][:, :],
                                                                    in1=x1[:, sub, hf * 512:(hf + 1) * 512], op=ALU.add),
                             reads=(f"po{hf}", f"x1_{sub}_{hf}"), writes=(f"xo_{hf}",))
                    P.op("act", lambda e: e.activation(out=xn2[:, :], in_=xo[:, :], func=AF.Square, accum_out=stat[:, 4:5]),
                         reads=("xo_0", "xo_1"), writes=("xn2", "ss4"))
                    P.op("act", lambda e: e.activation(out=stat[:, 5:6], in_=stat[:, 4:5], func=AF.Ln, scale=1.0 / D, bias=EPS), reads=("ss4",), writes=("ln4",))
                    P.op("act", lambda e: e.activation(out=stat[:, 6:7], in_=stat[:, 5:6], func=AF.Exp, scale=-0.5), reads=("ln4",), writes=("rstd4",))
                    P.op("dve", lambda e: e.scalar_tensor_tensor(out=xo[:, :], in0=xo[:, :], scalar=stat[:, 6:7], in1=gf[:, :], op0=ALU.mult, op1=ALU.mult),
                         reads=("xo_0", "xo_1", "rstd4", "gf"), writes=("xo_0", "xo_1"))
                    P.dma("sp", out[T0:T0 + 128, :], xo[:, :], xosem, reads=("xo_0", "xo_1"))

                r0 = pre(0)
                for r_ in r0:
                    r_()
                bulk_loads()
                outproj(0, 0, po, ("po0", "po1"))
                outproj(0, 1, pg, ("pg0", "pg1"))
                chain(0, 0)
                chain(0, 1)
                for t in range(NT4):
                    nxt = t + 1 < NT4
                    mloop(t, pre(t + 1) if nxt else ())
                    down(t, 0)
                    if nxt:
                        outproj(t + 1, 0, pg, ("pg0", "pg1"))
                    down(t, 1)
                    if nxt:
                        chain(t + 1, 0)
                        outproj(t + 1, 1, pu, ("pu0", "pu1"))
                        chain(t + 1, 1)
                P.barrier()
                P.emit()
    return nc


def _consts(S):
    f32 = np.float32
    inv_freq = (500000.0 ** (-(np.arange(0, 16, 2, dtype=f32) / f32(16)))).astype(f32)
    ang = (np.arange(S, dtype=f32)[:, None] * inv_freq[None, :]).astype(f32)
    cos, sin = np.cos(ang).astype(f32), np.sin(ang).astype(f32)
    C = np.ones((128, S), f32)
    Sg = np.zeros((128, S), f32)
    for p in range(128):
        d = p % 64
        if d < 8:
            C[p] = cos[:, d]
            Sg[p] = -sin[:, d]
        elif d < 16:
            C[p] = cos[:, d - 8]
            Sg[p] = sin[:, d - 8]
    k = np.arange(128)[:, None]
    q = np.arange(128)[None, :]
    ident = (k == q).astype(f32)
    maskc = np.where(k <= q, 0.0, NEGM).astype(f32)
    maskd = np.where((k >= 64) & (q < 64), NEGM, 0.0).astype(f32)
    cbf = np.concatenate([ident, maskc, maskd], axis=1).astype(ml_dtypes.bfloat16)
    pc = np.zeros((128, 6), f32)
    pc[64:70, 0] = (0, 1, 1, 0, 1, 1)
    pc[64:70, 1] = (0, 0, 1, 0, 0, 1)
    pc[64:70, 2] = (1, 1, 1, 0, 0, 0)
    pc[64:70, 3] = (0, 0, 0, 1, 1, 1)
    pc[64:70, 4] = (0, 0, 0, -1, -1, -1)
    pc[64:70, 5] = (1, 1, 1, 0, 0, 0)
    swf = (k == (q + 64) % 128).astype(f32)
    return C, Sg, cbf, pc, swf


def _group_cols(j):
    def swap(base):
        cols = []
        for p in range(128):
            c_, d = divmod(p, 64)
            pd = d + 8 if d < 8 else (d - 8 if d < 16 else d)
            cols.append(base + j * 128 + c_ * 64 + pd)
        return cols

    cols = list(range(j * 128, (j + 1) * 128)) + swap(0)
    cols += list(range(512 + j * 128, 512 + (j + 1) * 128)) + swap(512)
    h0, h1 = 2 * j, 2 * j + 1
    cols += list(range(1536 + h0 * 64, 1536 + h0 * 64 + 64)) + list(range(1536 + h1 * 64, 1536 + h1 * 64 + 64))
    cols += list(range(2048 + h0 * 64, 2048 + h0 * 64 + 64)) + [3072 + h0] * 6
    cols += list(range(2048 + h1 * 64, 2048 + h1 * 64 + 64)) + [3072 + h1] * 6
    cols += list(range(1024 + j * 128, 1024 + (j + 1) * 128))
    cols += list(range(2560 + h0 * 64, 2560 + h0 * 64 + 64)) + list(range(2560 + h1 * 64, 2560 + h1 * 64 + 64))
    assert len(cols) == NCOL
    return np.array(cols)


def make_in_maps(x, norm1_g, w_in, b_f, lam_q1, lam_k1, lam_q2, lam_k2, subln_g, w_out, norm2_g, w_gate, w_up, w_down, normf_g):
    f32 = np.float32
    a = lambda t: np.ascontiguousarray(np.asarray(t, dtype=f32))
    x = a(x)
    B, S, _ = x.shape
    TOKC = S // 4
    C, Sg, cbf, pc, swf = _consts(S)
    w_in0, w_out0 = a(w_in)[0], a(w_out)[0]
    perm = np.concatenate([np.concatenate([np.arange(r * 128, (r + 1) * 128), np.arange(512 + r * 128, 512 + (r + 1) * 128)]) for r in range(4)])
    w_out_p = np.ascontiguousarray(w_out0[perm])
    g1T = np.ascontiguousarray(a(norm1_g)[0].reshape(8, 128).T)
    g2T = np.ascontiguousarray(a(norm2_g)[0].reshape(8, 128).T)
    lamv = np.ascontiguousarray(np.broadcast_to(np.concatenate([a(lam_q1)[0], a(lam_k1)[0], a(lam_q2)[0], a(lam_k2)[0]])[None, :], (128, 256)))
    sgv = np.ascontiguousarray(a(subln_g)[0].reshape(128, 1))
    gfb = np.ascontiguousarray(np.broadcast_to(a(normf_g)[None, :], (128, D)))
    wg, wu, wd = a(w_gate)[0], a(w_up)[0], a(w_down)[0]
    bf0 = a(b_f)[0]
    maps = []
    for c in range(8):
        b, j = divmod(c, 4)
        bfp = np.zeros((128, 2), f32)
        bfp[64:70, 0] = bf0[2 * j]
        bfp[64:70, 1] = bf0[2 * j + 1]
        oh = np.zeros((128, 4), f32)
        oh[:, j] = 1.0
        maps.append({
            "xb": x[b], "xres": np.ascontiguousarray(x[b, j * TOKC:(j + 1) * TOKC]),
            "w_in": np.ascontiguousarray(w_in0[:, _group_cols(j)]),
            "g1T": g1T, "g2T": g2T, "bfp": bfp, "lamv": lamv, "sgv": sgv, "pcv": pc, "ohv": oh, "gfb": gfb,
            "ropeC": C, "ropeS": Sg, "cbf": cbf, "swfd": swf, "w_out": w_out_p, "w_gate": wg, "w_up": wu, "w_down": wd,
        })
    return maps, B, S


_NC_CACHE = {}


def kernel(**inputs):
    maps, B, S = make_in_maps(**inputs)
    if S not in _NC_CACHE:
        _NC_CACHE[S] = build(S)
    res = run_bass_kernel_spmd(_NC_CACHE[S], maps, core_ids=list(range(8)))
    TOKC = S // 4
    outp = np.empty((B, S, D), np.float32)
    for c in range(8):
        b, j = divmod(c, 4)
        outp[b, j * TOKC:(j + 1) * TOKC] = res.results[c]["out"]
    return outp
```
